# Optimizing a Trainium2 kernel written in Bass

```python
import math
import jax, jax.numpy as jnp
from jax import lax
import numpy as np

D_MODEL = 2048
BATCH = 1
SEQ = 16384
DEPTH = 1

HEAD_DIM = 128
N_DIFF_HEADS = D_MODEL // (4 * HEAD_DIM)
DIFF_V_DIM = 2 * HEAD_DIM
N_DIL_HEADS = D_MODEL // (2 * HEAD_DIM)
DIFF_QK_COLS = N_DIFF_HEADS * 2 * HEAD_DIM
DIFF_WIDTH = N_DIFF_HEADS * DIFF_V_DIM
DIL_WIDTH = N_DIL_HEADS * HEAD_DIM
MIX_WIDTH = DIFF_WIDTH + DIL_WIDTH
IN_PROJ_COLS = 2 * DIFF_QK_COLS + DIFF_WIDTH + 3 * DIL_WIDTH
DILATED_PATTERNS = ((128, 1), (512, 4), (2048, 16))
N_REL_BUCKETS = 32
REL_MAX_DISTANCE = 1024
D_FF = 5632
CONV_WIDTH = 3
Q_BLOCK = 128
NORM_EPS = 1e-6
SUBLN_EPS = 1e-5
NEG_INF = -1e30

kernel_name = "hybrid_diffattn_dilated_window_convffn_encoder"


def rmsnorm(x, gain, eps=NORM_EPS):
    xf = x.astype(jnp.float32)
    y = xf * lax.rsqrt(jnp.mean(xf * xf, axis=-1, keepdims=True) + eps)
    return (y * gain.astype(jnp.float32)).astype(x.dtype)


def rel_bucket(rel):
    nb = N_REL_BUCKETS // 2
    ret = jnp.where(rel > 0, nb, 0)
    n = jnp.abs(rel)
    max_exact = nb // 2
    nf = jnp.maximum(n, 1).astype(jnp.float32)
    large = max_exact + (jnp.log(nf / max_exact) / math.log(REL_MAX_DISTANCE / max_exact)
                         * (nb - max_exact)).astype(jnp.int32)
    large = jnp.minimum(large, nb - 1)
    return ret + jnp.where(n < max_exact, n, large)


def diff_attention(q1, q2, k1, k2, v, lam, table):
    B, H, S, D = q1.shape
    nblk = S // Q_BLOCK
    scale = 1.0 / math.sqrt(D)
    q1b = q1.reshape(B, H, nblk, Q_BLOCK, D).transpose(2, 0, 1, 3, 4)
    q2b = q2.reshape(B, H, nblk, Q_BLOCK, D).transpose(2, 0, 1, 3, 4)
    starts = jnp.arange(nblk, dtype=jnp.int32) * Q_BLOCK
    kpos = jnp.arange(S, dtype=jnp.int32)

    def block(args):
        q1_blk, q2_blk, start = args
        qpos = start + jnp.arange(Q_BLOCK, dtype=jnp.int32)
        bucket = rel_bucket(kpos[None, :] - qpos[:, None])
        bias = jnp.take(table, bucket, axis=0).transpose(2, 0, 1)[None]
        bias = bias.astype(jnp.float32)
        s1 = jnp.einsum('bhqd,bhkd->bhqk', q1_blk, k1).astype(jnp.float32) * scale + bias
        s2 = jnp.einsum('bhqd,bhkd->bhqk', q2_blk, k2).astype(jnp.float32) * scale + bias
        attn = jax.nn.softmax(s1, axis=-1) - lam * jax.nn.softmax(s2, axis=-1)
        return jnp.einsum('bhqk,bhkv->bhqv', attn.astype(v.dtype), v)

    out = lax.map(block, (q1b, q2b, starts))
    return out.transpose(1, 2, 0, 3, 4).reshape(B, H, S, v.shape[-1])


def dilated_window(q, k, v, window, dilation, table):
    B, S, H, D = q.shape
    half = window // (2 * dilation)
    L = S // dilation
    nb = -(-L // half)
    Lp = nb * half
    scale = 1.0 / math.sqrt(D)

    def to_sub(t):
        return t.reshape(B, L, dilation, H, D).transpose(0, 2, 3, 1, 4)

    def windows(t):
        tp = jnp.pad(to_sub(t), ((0, 0), (0, 0), (0, 0), (half, Lp - L + half), (0, 0)))
        tp = tp.reshape(B, dilation, H, nb + 2, half, D)
        return jnp.concatenate([tp[:, :, :, :-2], tp[:, :, :, 1:-1], tp[:, :, :, 2:]], axis=4)

    qs = jnp.pad(to_sub(q), ((0, 0), (0, 0), (0, 0), (0, Lp - L), (0, 0)))
    qs = qs.reshape(B, dilation, H, nb, half, D)
    kw = windows(k)
    vw = windows(v)

    a = jnp.arange(half, dtype=jnp.int32)[:, None]
    b = jnp.arange(3 * half, dtype=jnp.int32)[None, :]
    off = b - half - a
    band = jnp.abs(off) <= half
    key_idx = jnp.arange(nb, dtype=jnp.int32)[:, None] * half - half + jnp.arange(3 * half, dtype=jnp.int32)[None, :]
    inrange = (key_idx >= 0) & (key_idx < L)
    mask = band[None] & inrange[:, None, :]
    bias = jnp.take(table, rel_bucket(off * dilation), axis=0).transpose(2, 0, 1).astype(jnp.float32)

    s = jnp.einsum('bchnqd,bchnkd->bchnqk', qs, kw).astype(jnp.float32) * scale + bias[None, None, :, None]
    s = jnp.where(mask[None, None, None], s, NEG_INF)
    m = jnp.max(s, axis=-1, keepdims=True)
    e = jnp.exp(s - m)
    den = jnp.sum(e, axis=-1, keepdims=True)
    o = jnp.einsum('bchnqk,bchnkd->bchnqd', (e / den).astype(v.dtype), vw)
    lse = (m + jnp.log(den))[..., 0]

    o = o.reshape(B, dilation, H, Lp, D)[:, :, :, :L].transpose(0, 3, 1, 2, 4).reshape(B, S, H, D)
    lse = lse.reshape(B, dilation, H, Lp)[:, :, :, :L].transpose(0, 3, 1, 2).reshape(B, S, H)
    return o, lse


def depthwise_conv(h, w, bias):
    C = h.shape[-1]
    pad = CONV_WIDTH // 2
    y = lax.conv_general_dilated(h, w.astype(h.dtype)[:, None, :], window_strides=(1,),
                                 padding=((pad, pad),), dimension_numbers=('NWC', 'WIO', 'NWC'),
                                 feature_group_count=C)
    return y + bias.astype(h.dtype)


def setup_inputs(seed: int = 0) -> dict:
    key = jax.random.key(seed)
    ks = jax.random.split(key, 20)
    f32 = jnp.float32

    def nrm(k, shape, scale):
        return jax.random.normal(k, shape, f32) * scale

    return {
        "x": nrm(ks[0], (BATCH, SEQ, D_MODEL), 1.0),
        "norm1_gain": 1.0 + nrm(ks[1], (DEPTH, D_MODEL), 0.02),
        "w_in": nrm(ks[2], (DEPTH, D_MODEL, IN_PROJ_COLS), D_MODEL ** -0.5),
        "rel_bias_table": nrm(ks[3], (N_REL_BUCKETS, N_DIFF_HEADS + N_DIL_HEADS), 0.2),
        "lambda_q1": nrm(ks[4], (DEPTH, HEAD_DIM), 0.1),
        "lambda_k1": nrm(ks[5], (DEPTH, HEAD_DIM), 0.1),
        "lambda_q2": nrm(ks[6], (DEPTH, HEAD_DIM), 0.1),
        "lambda_k2": nrm(ks[7], (DEPTH, HEAD_DIM), 0.1),
        "diff_subln_gain": 1.0 + nrm(ks[8], (DEPTH, DIFF_V_DIM), 0.02),
        "dil_out_gain": 1.0 + nrm(ks[9], (DEPTH, DIL_WIDTH), 0.02),
        "w_out": nrm(ks[10], (DEPTH, MIX_WIDTH, D_MODEL), MIX_WIDTH ** -0.5),
        "norm2_gain": 1.0 + nrm(ks[11], (DEPTH, D_MODEL), 0.02),
        "w_gate_up": nrm(ks[12], (DEPTH, D_MODEL, 2 * D_FF), D_MODEL ** -0.5),
        "conv_w": nrm(ks[13], (DEPTH, CONV_WIDTH, D_FF), CONV_WIDTH ** -0.5),
        "conv_b": nrm(ks[14], (DEPTH, D_FF), 0.01),
        "w_down": nrm(ks[15], (DEPTH, D_FF, D_MODEL), D_FF ** -0.5),
        "final_gain": 1.0 + nrm(ks[16], (D_MODEL,), 0.02),
    }


def reference(x, norm1_gain, w_in, rel_bias_table, lambda_q1, lambda_k1, lambda_q2, lambda_k2,
              diff_subln_gain, dil_out_gain, w_out, norm2_gain, w_gate_up, conv_w, conv_b,
              w_down, final_gain):
    B, S, _ = x.shape
    table_diff = rel_bias_table[:, :N_DIFF_HEADS]
    table_dil = rel_bias_table[:, N_DIFF_HEADS:]
    split_at = [int(c) for c in np.cumsum([DIFF_QK_COLS, DIFF_QK_COLS, DIFF_WIDTH, DIL_WIDTH, DIL_WIDTH])]

    for l in range(DEPTH):
        h = rmsnorm(x, norm1_gain[l])
        proj = h @ w_in[l]
        dq, dk, dv, lq, lk, lv = jnp.split(proj, split_at, axis=-1)

        dq = dq.reshape(B, S, N_DIFF_HEADS, 2, HEAD_DIM)
        dk = dk.reshape(B, S, N_DIFF_HEADS, 2, HEAD_DIM)
        q1 = dq[:, :, :, 0].transpose(0, 2, 1, 3)
        q2 = dq[:, :, :, 1].transpose(0, 2, 1, 3)
        k1 = dk[:, :, :, 0].transpose(0, 2, 1, 3)
        k2 = dk[:, :, :, 1].transpose(0, 2, 1, 3)
        v_d = dv.reshape(B, S, N_DIFF_HEADS, DIFF_V_DIM).transpose(0, 2, 1, 3)
        lam_init = 0.8 - 0.6 * math.exp(-0.3 * l)
        lam = (jnp.exp(jnp.sum(lambda_q1[l].astype(jnp.float32) * lambda_k1[l].astype(jnp.float32)))
               - jnp.exp(jnp.sum(lambda_q2[l].astype(jnp.float32) * lambda_k2[l].astype(jnp.float32)))
               + lam_init)
        o_d = diff_attention(q1, q2, k1, k2, v_d, lam, table_diff)
        o_d = rmsnorm(o_d, diff_subln_gain[l], SUBLN_EPS) * (1.0 - lam_init)
        o_d = o_d.transpose(0, 2, 1, 3).reshape(B, S, DIFF_WIDTH)

        q = lq.reshape(B, S, N_DIL_HEADS, HEAD_DIM)
        k = lk.reshape(B, S, N_DIL_HEADS, HEAD_DIM)
        v = lv.reshape(B, S, N_DIL_HEADS, HEAD_DIM)
        outs, lses = [], []
        for window, dilation in DILATED_PATTERNS:
            o_p, lse_p = dilated_window(q, k, v, window, dilation, table_dil)
            outs.append(o_p)
            lses.append(lse_p)
        wts = jax.nn.softmax(jnp.stack(lses, axis=0), axis=0)
        o_l = jnp.sum(wts[..., None].astype(v.dtype) * jnp.stack(outs, axis=0), axis=0)
        o_l = rmsnorm(o_l, dil_out_gain[l].reshape(N_DIL_HEADS, HEAD_DIM))
        o_l = o_l.reshape(B, S, DIL_WIDTH)

        x = x + jnp.concatenate([o_d, o_l], axis=-1) @ w_out[l]

        h2 = rmsnorm(x, norm2_gain[l])
        g, u = jnp.split(h2 @ w_gate_up[l], 2, axis=-1)
        g = depthwise_conv(g, conv_w[l], conv_b[l])
        x = x + (jax.nn.silu(g) * u) @ w_down[l]

    return rmsnorm(x, final_gain)
```

```python
import math
import os
from contextlib import ExitStack
import numpy as np
import ml_dtypes
import concourse.bass as bass
import concourse.mybir as mybir
from concourse.bass_utils import run_bass_kernel_spmd

F32 = mybir.dt.float32
BF16 = mybir.dt.bfloat16
AF = mybir.ActivationFunctionType
ALU = mybir.AluOpType
AX = mybir.AxisListType

S = 16384
DM = 2048
NCORE = 8
OWN = 2048
EXT = 2304
NET = 18
SCALE = 1.0 / math.sqrt(128.0)
J0 = 942
TZW = 2048
DKN = 4352
NFC = 44
NEG = -1e30
PATS = ((128, 1), (512, 4), (2048, 16))


class Res:
    __slots__ = ("w", "rs", "pend")

    def __init__(self):
        self.w = None
        self.rs = {}
        self.pend = None


class _Eng:
    pass


class _DSem:
    pass


class Sch:
    LIMIT = 30000

    def __init__(self, nc):
        self.nc = nc
        self.E = {}
        self.nsem = 0
        self.dsems = []
        for n, h in (("pe", nc.tensor), ("act", nc.scalar), ("dve", nc.vector),
                     ("pool", nc.gpsimd), ("sp", nc.sync)):
            e = _Eng()
            e.h = h
            e.name = n
            e.sem = self._newsem("e_" + n)
            e.cnt = 0
            e.waited = {}
            e.deferred = []
            self.E[n] = e

    def _newsem(self, name):
        self.nsem += 1
        return self.nc.alloc_semaphore(f"{name}_{self.nsem}")

    def dsem(self):
        d = _DSem()
        d.h = self._newsem("d")
        d.cnt = 0
        self.dsems.append(d)
        return d

    def _wait(self, e, tok):
        if tok is None:
            return
        sem, val = tok
        k = id(sem)
        if e.waited.get(k, (None, 0))[1] >= val:
            return
        e.h.wait_ge(sem, val)
        e.waited[k] = (sem, val)

    def _gather(self, e, reads, writes):
        for r in reads:
            assert r.pend is None or r.pend is e, "read of pending resource"
            self._wait(e, r.w)
        for w in writes:
            assert w.pend is None or w.pend is e, "write of pending resource"
            self._wait(e, w.w)
            for t in w.rs.values():
                self._wait(e, t)

    @staticmethod
    def _commit(tok, reads, writes):
        for r in reads:
            r.rs[id(tok[0])] = tok
            r.pend = None
        for w in writes:
            w.w = tok
            w.rs = {}
            w.pend = None

    def op(self, eng, fn, reads=(), writes=(), defer=False):
        e = self.E[eng]
        self._gather(e, reads, writes)
        ins = fn(e.h)
        if defer:
            e.deferred.append((tuple(reads), tuple(writes)))
            for r in reads:
                r.pend = e
            for w in writes:
                w.pend = e
            return
        if e.cnt >= self.LIMIT:
            e.sem = self._newsem("e_" + e.name)
            e.cnt = 0
        e.cnt += 1
        ins.then_inc(e.sem, 1)
        tok = (e.sem, e.cnt)
        for rr, ww in e.deferred:
            self._commit(tok, rr, ww)
        e.deferred = []
        self._commit(tok, reads, writes)

    def dma(self, q, out, in_, reads, writes, sem, slow=False):
        e = self.E[q]
        self._gather(e, reads, writes)
        if sem.cnt + 16 > self.LIMIT:
            sem.h = self._newsem("d")
            sem.cnt = 0
        ins = e.h.dma_start(out=out, in_=in_, allow_slow_non_contiguous=True) if slow else e.h.dma_start(out=out, in_=in_)
        sem.cnt += 16
        ins.then_inc(sem.h, 16)
        self._commit((sem.h, sem.cnt), reads, writes)

    def barrier(self):
        toks = []
        for e in self.E.values():
            assert not e.deferred
            if e.cnt > 0:
                toks.append((e.sem, e.cnt))
        for d in self.dsems:
            if d.cnt > 0:
                toks.append((d.h, d.cnt))
        for e in self.E.values():
            for t in toks:
                self._wait(e, t)


def _bucket(rel):
    rel = np.asarray(rel, dtype=np.int64)
    ret = np.where(rel > 0, 16, 0)
    n = np.abs(rel)
    nf = np.maximum(n, 1).astype(np.float32)
    large = 8 + (np.log(nf / np.float32(8)) / np.float32(math.log(128.0)) * np.float32(8)).astype(np.int32)
    large = np.minimum(large, 15)
    return ret + np.where(n < 8, n, large)


def _diff_tile_kind(kb, qb):
    D = 128 * kb - (384 * qb - 128)
    if kb >= 120 and (D - S) > -686:
        return 1, J0 - (D - S)
    if -686 < D < 942:
        return (0 if kb < 16 else 2), J0 - D
    return None, None


def _dil_chunks():
    base = {}
    gi = 0
    for p, (_, d) in enumerate(PATS):
        nq = EXT // d
        nblk = -(-nq // 128)
        for c in range(d):
            base[(p, c)] = gi
            gi += nblk + 1
    return base, gi


def _host_tables(core, table):
    tdiff = table[:, :4]
    tdil = table[:, 4:]
    p = np.arange(128)[:, None]
    j = np.arange(TZW)[None, :]
    rho = p - j + J0
    bk = _bucket(rho)
    tz = np.zeros((4, 128, 3, TZW), np.float32)
    for h in range(4):
        f = tdiff[bk, h]
        tz[h, :, 0] = f
        tz[h, :, 1] = f if core >= 1 else tdiff[31, h]
        tz[h, :, 2] = f if core < 7 else tdiff[15, h]
    cbv = np.zeros((4, 128, 6 * 128), np.float32)
    for qb in range(6):
        tq = (384 * qb - 128 + 192) + 2048 * core
        for kb in range(128):
            tk = (128 * kb + 64 + 2048 * core) % S
            b = 31 if tk > tq else 15
            cbv[:, :, qb * 128 + kb] = tdiff[b, :][:, None]
    bm = np.full((8, 128, 3, 2, 128), NEG, np.float32)
    lane = np.arange(128)[:, None]
    q = np.arange(128)[None, :]
    for pi, (_, d) in enumerate(PATS):
        for ch in range(2):
            off = lane - 64 - q if ch == 0 else lane + 64 - q
            ok = np.abs(off) <= 64
            bkt = _bucket(off * d)
            for h in range(8):
                bm[h, :, pi, ch, :] = np.where(ok, tdil[bkt, h], NEG)
    base, ntot = _dil_chunks()
    kv = np.zeros((128, ntot), np.float32)
    for pi, (_, d) in enumerate(PATS):
        nq = EXT // d
        nblk = -(-nq // 128)
        for c in range(d):
            for i in range(nblk + 1):
                dk = c + d * (128 * i - 64 + np.arange(128)) + 1024
                t = dk - 1152 + 2048 * core
                kv[:, base[(pi, c)] + i] = np.where((t >= 0) & (t < S), 0.0, NEG)
    return tz, cbv, bm, kv


def build_nc():
    nc = bass.Bass("TRN2", target_bir_lowering=False)
    sc = Sch(nc)

    def din(name, shape, dt=F32):
        return nc.dram_tensor(name, list(shape), dt, kind="ExternalInput").ap()

    def dscr(name, shape, dt=BF16):
        return nc.dram_tensor(name, list(shape), dt).ap()

    base, NCH = _dil_chunks()
    MISC = 512 + 256 + 8 + NFC * 3 + NFC + NET
    xr = din("xr", [S, DM])
    gains = din("gains", [3, 128, DM])
    wink = din("wink", [32, 128, 16 * 128])
    winv = din("winv", [2, 128, 16 * 512])
    winlv = din("winlv", [2, 128, 16 * 512])
    wout = din("wout", [128, 16 * DM])
    wgu = din("wgu", [88, 128, 16 * 128])
    wdn = din("wdn", [8, 128, NFC * 256])
    tzi = din("tz", [4, 128, 3 * TZW])
    cbi = din("cbv", [4, 128, 768])
    bmi = din("bm", [8, 128, 3 * 2 * 128])
    kvi = din("kv", [128, NCH])
    misc = din("misc", [128, MISC])
    identi = din("ident", [128, 128])
    y = nc.dram_tensor("y", [OWN, DM], F32, kind="ExternalOutput").ap()

    wink_b = dscr("wink_b", [32, 128, 16 * 128])
    winv_b = dscr("winv_b", [2, 128, 16 * 512])
    winlv_b = dscr("winlv_b", [2, 128, 16 * 512])
    wout_b = dscr("wout_b", [128, 16 * DM])
    wgu_b = dscr("wgu_b", [88, 128, 16 * 128])
    wdn_b = dscr("wdn_b", [8, 128, NFC * 256])
    hk = dscr("hk", [8, 128, S])
    hv = dscr("hv", [4, S, 256])
    qd = dscr("qd", [8, 128, EXT])
    lk = dscr("lk", [8, 128, DKN])
    lv = dscr("lv", [8, DKN + 2048, 128])
    ql = dscr("ql", [8, 128, EXT])
    odT = dscr("odT", [16, 128, EXT])
    x1s = dscr("x1s", [EXT, DM], F32)
    h2s = dscr("h2s", [16, 128, EXT + 2])

    R = Res
    pb = [nc.alloc_psum_tensor(f"pb{i}", [128, 512], F32) for i in range(6)]
    Rpb = [R() for _ in range(6)]
    tp = nc.alloc_psum_tensor("tp", [128, 2048], BF16)
    Rtp = R()

    ident = nc.alloc_sbuf_tensor("ident_sb", [128, 128], BF16)
    misc_sb = nc.alloc_sbuf_tensor("misc_sb", [128, MISC], F32)
    neglam = nc.alloc_sbuf_tensor("neglam", [128, 1], F32)
    subg8 = nc.alloc_sbuf_tensor("subg8", [128, 256], F32)
    ones_bf = nc.alloc_sbuf_tensor("ones_bf", [128, 128], BF16)
    ones_f = nc.alloc_sbuf_tensor("ones_f", [128, 128], F32)
    zeros_bf = nc.alloc_sbuf_tensor("zeros_bf", [128, 16], BF16)
    sm = nc.alloc_sbuf_tensor("sm", [128, 64], F32)
    Rc = R()
    Rsm = [R() for _ in range(8)]
    O_LAM, O_SUBG, O_DILG, O_CW, O_CB, O_TM = 0, 512, 768, 776, 776 + NFC * 3, 776 + NFC * 4

    ds_w = sc.dsem()
    ds_c = sc.dsem()
    sc.dma("pool", ident[:, :], identi[:, :], [], [Rc], ds_c)
    sc.dma("sp", misc_sb[:, :], misc[:, :], [], [Rc], ds_c)
    sc.op("pool", lambda h: h.memset(ones_bf[:, :], 1.0), [], [Rc])
    sc.op("pool", lambda h: h.memset(ones_f[:, :], 1.0), [], [Rc])
    sc.op("pool", lambda h: h.memset(zeros_bf[:, :], 0.0), [], [Rc])
    lt = nc.alloc_sbuf_tensor("lt", [128, 2, 128], F32)
    sc.op("dve", lambda h: h.tensor_tensor(out=lt[:, 0, :], in0=misc_sb[:, 0:128], in1=misc_sb[:, 128:256], op=ALU.mult), [Rc], [Rsm[0]])
    sc.op("dve", lambda h: h.tensor_tensor(out=lt[:, 1, :], in0=misc_sb[:, 256:384], in1=misc_sb[:, 384:512], op=ALU.mult), [Rc], [Rsm[1]])
    sc.op("dve", lambda h: h.reduce_sum(out=sm[:, 0:1], in_=lt[:, 0, :], axis=AX.X), [Rsm[0]], [Rsm[2]])
    sc.op("dve", lambda h: h.reduce_sum(out=sm[:, 1:2], in_=lt[:, 1, :], axis=AX.X), [Rsm[1]], [Rsm[3]])
    sc.op("act", lambda h: h.activation(out=sm[:, 2:3], in_=sm[:, 0:1], func=AF.Exp), [Rsm[2]], [Rsm[4]])
    sc.op("act", lambda h: h.activation(out=sm[:, 3:4], in_=sm[:, 1:2], func=AF.Exp), [Rsm[3]], [Rsm[5]])
    sc.op("dve", lambda h: h.tensor_tensor(out=sm[:, 4:5], in0=sm[:, 3:4], in1=sm[:, 2:3], op=ALU.subtract), [Rsm[4], Rsm[5]], [Rsm[6]])
    sc.op("dve", lambda h: h.tensor_scalar(out=neglam[:, :], in0=sm[:, 4:5], scalar1=-0.2, scalar2=None, op0=ALU.add), [Rsm[6]], [Rc])
    sc.op("dve", lambda h: h.tensor_scalar(out=subg8[:, :], in0=misc_sb[:, O_SUBG:O_SUBG + 256], scalar1=0.8, scalar2=None, op0=ALU.mult), [Rc], [Rc])

    def cast_w(dst, src, n, width, R_list):
        for b in range(n):
            for c0 in range(0, width, 2048):
                c1 = min(width, c0 + 2048)
                sc.dma("pool", dst[b, :, c0:c1], src[b, :, c0:c1], [], [R_list[b]], ds_w)

    Rwink = [R() for _ in range(32)]
    Rwinv = [R() for _ in range(2)]
    Rwinlv = [R() for _ in range(2)]
    Rwout = [R()]
    Rwgu = [R() for _ in range(88)]
    Rwdn = [R() for _ in range(8)]
    cast_w(winv_b, winv, 2, 16 * 512, Rwinv)
    cast_w(wink_b, wink, 32, 16 * 128, Rwink)
    cast_w(winlv_b, winlv, 2, 16 * 512, Rwinlv)

    def rstd_chain(ss_ap, out_ap, inv_n, eps, Rin, Rout, tmp):
        a, b = tmp
        sc.op("dve", lambda h: h.tensor_scalar(out=sm[:, a:a + 1], in0=ss_ap, scalar1=inv_n, scalar2=eps, op0=ALU.mult, op1=ALU.add), [Rin], [Rsm[6]])
        sc.op("act", lambda h: h.activation(out=sm[:, b:b + 1], in_=sm[:, a:a + 1], func=AF.Sqrt), [Rsm[6]], [Rsm[7]])
        sc.op("dve", lambda h: h.reciprocal(out=out_ap, in_=sm[:, b:b + 1]), [Rsm[7]], [Rout])

    ds_x = sc.dsem()
    ds_ld = sc.dsem()
    ds_st = sc.dsem()
    Rscr = {}

    def rs(key):
        if key not in Rscr:
            Rscr[key] = R()
        return Rscr[key]

    es = ExitStack()

    def A(name, shape, dt):
        return es.enter_context(nc.sbuf_tensor(name, shape, dt))

    P1 = {}
    P1["g1"] = A("g1", [128, DM], F32)
    Rg1 = R()
    sc.dma("sp", P1["g1"][:, :], gains[0], [], [Rg1], ds_c)
    hTb = [A(f"hT{i}", [128, 16, 1024], BF16) for i in range(2)]
    RhT = [[R() for _ in range(8)] for _ in range(2)]
    xtb = [A(f"xt{i}", [128, DM], F32) for i in range(2)]
    Rxt = [R(), R()]
    xnb = [A(f"xn{i}", [128, DM], BF16) for i in range(2)]
    Rxn = [R(), R()]
    junk = A("junk", [128, DM], BF16)
    Rjunk = R()
    ssb = A("ssb", [128, 4], F32)
    Rss = [R(), R()]
    Rrs = [R(), R()]
    wkt = [A(f"wkt{i}", [128, 16, 128], BF16) for i in range(3)]
    Rwkt = [R() for _ in range(3)]
    wvt = [A(f"wvt{i}", [128, 16, 512], BF16) for i in range(2)]
    Rwvt = [R() for _ in range(2)]
    stg = [A(f"stg{i}", [128, 512], BF16) for i in range(4)]
    Rstg = [R() for _ in range(4)]
    cnt = {"x": 0, "wk": 0, "wv": 0, "stg": 0, "pb": 0}

    def next_pb():
        i = cnt["pb"] % 6
        cnt["pb"] += 1
        return i

    def evac_store(psap, n_free_shape, dst_ap, Rp, Rdst, use_act):
        i = cnt["stg"] % 4
        cnt["stg"] += 1
        st_ap = n_free_shape(stg[i])
        if use_act:
            sc.op("act", lambda h: h.copy(out=st_ap, in_=psap), [Rp], [Rstg[i]])
        else:
            sc.op("dve", lambda h: h.tensor_copy(out=st_ap, in_=psap), [Rp], [Rstg[i]])
        sc.dma("pool", dst_ap, st_ap, [Rstg[i]], [Rdst], ds_st)

    def norm_tile(src_rows, xt, Rx, gain_t, Rg, xn, Rn, i2, extra_scale=None):
        sc.op("act", lambda h: h.activation(out=junk[:, :], in_=xt[:, :], func=AF.Square, accum_out=ssb[:, i2:i2 + 1]), [Rx], [Rjunk, Rss[i2]])
        rstd_chain(ssb[:, i2:i2 + 1], ssb[:, 2 + i2:3 + i2], 1.0 / DM, 1e-6, Rss[i2], Rrs[i2], (8, 9))
        if extra_scale is not None:
            sc.op("dve", lambda h: h.tensor_tensor(out=ssb[:, 2 + i2:3 + i2], in0=ssb[:, 2 + i2:3 + i2], in1=extra_scale, op=ALU.mult), [Rrs[i2], Rc], [Rrs[i2]])
        sc.op("dve", lambda h: h.scalar_tensor_tensor(out=xn[:, :], in0=xt[:, :], scalar=ssb[:, 2 + i2:3 + i2], in1=gain_t[:, :], op0=ALU.mult, op1=ALU.mult), [Rx, Rrs[i2], Rg], [Rn])

    def transpose16(xn, Rn, dst_ap, Rdst_list):
        for k in range(16):
            sc.op("pe", lambda h, k=k: h.transpose(out=tp[:, k * 128:(k + 1) * 128], in_=xn[:, k * 128:(k + 1) * 128], identity=ident[:, :]),
                  [Rn, Rc], [Rtp], defer=(k < 15))
        sc.op("act", lambda h: h.copy(out=dst_ap, in_=tp[:, :].rearrange("p (k c) -> p k c", k=16)), [Rtp], Rdst_list)

    def tile_groups(ta, tb):
        g = []
        t = ta
        while t < tb:
            e = min(tb, t + 4)
            g.append((t, e))
            t = e
        return g

    def dkcol(T):
        return (T - 119) * 128 if T >= 119 else (T + 9) * 128

    def ecol(T):
        return 0 if T == 127 else (T + 1) * 128

    def norm_items(ST):
        hb = ST % 2
        hT = hTb[hb]
        items = []
        for t in range(8):
            def f(t=t):
                T = ST * 8 + t
                i2 = cnt["x"] % 2
                cnt["x"] += 1
                sc.dma("sp", xtb[i2][:, :], xr[T * 128:(T + 1) * 128, :], [], [Rxt[i2]], ds_x)
                norm_tile(None, xtb[i2], Rxt[i2], P1["g1"], Rg1, xnb[i2], Rxn[i2], i2)
                transpose16(xnb[i2], Rxn[i2], hT[:, :, t * 128:(t + 1) * 128], [RhT[hb][t]])
            items.append(f)
        return items

    def mm_items(ST):
        hb = ST % 2
        hT = hTb[hb]
        items = []

        def ktype(blk, dst_fn, ta, tb):
            st_ = {}

            def ld():
                i = cnt["wk"] % 3
                cnt["wk"] += 1
                st_["i"] = i
                sc.dma("sp", wkt[i][:, :, :].rearrange("p k c -> p (k c)"), wink_b[blk], [Rwink[blk]], [Rwkt[i]], ds_ld)
            items.append(ld)
            for (a, b) in tile_groups(ta, tb):
                def grp(a=a, b=b):
                    i = st_["i"]
                    n = (b - a) * 128
                    pi = next_pb()
                    for k in range(16):
                        sc.op("pe", lambda h, k=k: h.matmul(pb[pi][:, 0:n], lhsT=wkt[i][:, k, :], rhs=hT[:, k, a * 128:b * 128], start=(k == 0), stop=(k == 15)),
                              [Rwkt[i]] + RhT[hb][a:b], [Rpb[pi]], defer=(k < 15))
                    dst, Rd = dst_fn(a, b)
                    evac_store(pb[pi][:, 0:n], lambda s_: s_[:, 0:n], dst, Rpb[pi], Rd, use_act=(cnt["stg"] % 2 == 0))
                items.append(grp)

        def vtype(wsrc, Rw, tiles, dst_fn, Rd):
            st_ = {}

            def ld():
                i = cnt["wv"] % 2
                cnt["wv"] += 1
                st_["i"] = i
                sc.dma("sp", wvt[i][:, :, :].rearrange("p k c -> p (k c)"), wsrc, [Rw], [Rwvt[i]], ds_ld)
            items.append(ld)
            for t in tiles:
                def tl(t=t):
                    i = st_["i"]
                    pi = next_pb()
                    for k in range(16):
                        sc.op("pe", lambda h, k=k: h.matmul(pb[pi][:, 0:512], lhsT=hT[:, k, t * 128:(t + 1) * 128], rhs=wvt[i][:, k, :], start=(k == 0), stop=(k == 15)),
                              [Rwvt[i], RhT[hb][t]], [Rpb[pi]], defer=(k < 15))
                    psap, shp, dst = dst_fn(ST * 8 + t, pb[pi])
                    evac_store(psap, shp, dst, Rpb[pi], Rd, use_act=(cnt["stg"] % 2 == 0))
                items.append(tl)

        for b8 in range(8):
            ktype(8 + b8, lambda a, b, b8=b8: (hk[b8, :, (ST * 8 + a) * 128:(ST * 8 + b) * 128], rs(("hk", b8))), 0, 8)

        def vdst(dram, nh, h0):
            return lambda T, p_: (p_[:, 0:512].rearrange("p (h c) -> p h c", h=nh), (lambda s_: s_[:, 0:512].rearrange("p (h c) -> p h c", h=nh)),
                                  dram(T)[h0:h0 + nh].rearrange("h r c -> r h c"))

        for u in range(2):
            vtype(winv_b[u], Rwinv[u], list(range(8)), vdst(lambda T: hv[:, T * 128:(T + 1) * 128, :], 2, 2 * u), rs(("hv", u)))
        dil_tiles = [t for t in range(8) if (ST * 8 + t) >= 119 or (ST * 8 + t) <= 24]
        q_tiles = [t for t in range(8) if (ST * 8 + t) == 127 or (ST * 8 + t) <= 16]
        if dil_tiles:
            ta, tb = dil_tiles[0], dil_tiles[-1] + 1
            for b8 in range(8):
                ktype(24 + b8, lambda a, b, b8=b8: (lk[b8, :, dkcol(ST * 8 + a):dkcol(ST * 8 + a) + (b - a) * 128], rs(("lk", b8))), ta, tb)
            for jj in range(2):
                vtype(winlv_b[jj], Rwinlv[jj], dil_tiles, vdst(lambda T: lv[:, dkcol(T):dkcol(T) + 128, :], 4, 4 * jj), rs(("lv", jj)))
        if q_tiles:
            ta, tb = q_tiles[0], q_tiles[-1] + 1
            for b8 in range(8):
                ktype(b8, lambda a, b, b8=b8: (qd[b8, :, ecol(ST * 8 + a):ecol(ST * 8 + a) + (b - a) * 128], rs(("qd", b8))), ta, tb)
                ktype(16 + b8, lambda a, b, b8=b8: (ql[b8, :, ecol(ST * 8 + a):ecol(ST * 8 + a) + (b - a) * 128], rs(("ql", b8))), ta, tb)
        return items

    NST = int(os.environ.get("NST", "16"))
    for f in norm_items(0):
        f()
    for ST in range(NST):
        mm = mm_items(ST)
        nx = norm_items(ST + 1) if ST + 1 < NST else []
        step = max(1, len(mm) // (len(nx) + 1)) if nx else len(mm) + 1
        ni = 0
        for idx, it in enumerate(mm):
            it()
            if nx and (idx + 1) % step == 0 and ni < len(nx):
                nx[ni]()
                ni += 1
        while ni < len(nx):
            nx[ni]()
            ni += 1
        if ST == 1:
            cast_w(wout_b.rearrange("p (o c) -> o p c", o=1), wout.rearrange("p (o c) -> o p c", o=1), 1, 16 * DM, Rwout)
        if ST == 3:
            cast_w(wgu_b, wgu, 88, 16 * 128, Rwgu)
        if ST == 5:
            cast_w(wdn_b, wdn, 8, NFC * 256, Rwdn)

    sc.barrier()
    es.close()

    es = ExitStack()
    KT = [A("KT0", [128, S], BF16), A("KT1", [128, S], BF16)]
    RKT = [R(), R()]
    vt = A("vt", [128, 128, 257], BF16)
    Rvt = R()
    QT = [A("QT0", [128, EXT], BF16), A("QT1", [128, EXT], BF16)]
    RQT = [R(), R()]
    tz_sb = A("tz_sb", [128, 3, TZW], F32)
    cb_sb = A("cb_sb", [128, 768], F32)
    Rtz = R()
    pT = [A(f"pT{i}", [128, 384], BF16) for i in range(4)]
    RpT = [R() for _ in range(4)]
    sbt = [A(f"sbt{i}", [128, 384], F32) for i in range(3)]
    Rsbt = [R() for _ in range(3)]
    Amap = [A(f"Amap{i}", [128, 3, 256], F32) for i in range(2)]
    RA = [[R() for _ in range(3)] for _ in range(2)]
    ot = A("ot", [128, 256], F32)
    Rot = R()
    onb = A("onb", [128, 256], BF16)
    Ronb = R()
    ost = [A(f"ost{i}", [128, 2, 128], BF16) for i in range(2)]
    Rost = [R(), R()]
    junk2 = A("junk2", [128, 256], BF16)
    Rj2 = R()
    sm2 = A("sm2", [128, 16], F32)
    Rs2 = [R() for _ in range(8)]
    sc.op("pool", lambda h: h.memset(vt[:, :, 256:257], 1.0), [], [Rvt])
    c2 = {"s": 0, "p": 0, "sb": 0, "o": 0}
    for hh in range(4):
        for m in range(2):
            sc.dma("sp", KT[m][:, :], hk[2 * hh + m], [rs(("hk", 2 * hh + m))], [RKT[m]], ds_ld)
            sc.dma("sp", QT[m][:, :], qd[2 * hh + m], [rs(("qd", 2 * hh + m))], [RQT[m]], ds_ld)
        for g8 in range(8):
            sc.dma("sp", vt[:, g8 * 16:(g8 + 1) * 16, 0:256], hv[hh, g8 * 2048:(g8 + 1) * 2048, :].rearrange("(kb p) c -> p kb c", p=128),
                   [rs(("hv", hh // 2))], [Rvt], ds_ld)
        sc.dma("sp", tz_sb[:, :, :].rearrange("p a b -> p (a b)"), tzi[hh], [], [Rtz], ds_ld)
        sc.dma("sp", cb_sb[:, :], cbi[hh], [], [Rtz], ds_ld)
        for qb in range(6):
            for m in range(2):
                qs = QT[m][:, qb * 384:(qb + 1) * 384]

                def qk(kb):
                    si = 3 + (c2["s"] % 3)
                    c2["s"] += 1
                    sc.op("pe", lambda h: h.matmul(pb[si][:, 0:384], lhsT=KT[m][:, kb * 128:(kb + 1) * 128], rhs=qs, start=True, stop=True),
                          [RKT[m], RQT[m]], [Rpb[si]])
                    return si

                def expo(kb, si):
                    pi_ = c2["p"] % 4
                    c2["p"] += 1
                    var, st = _diff_tile_kind(kb, qb)
                    if var is None:
                        sc.op("act", lambda h: h.activation(out=pT[pi_][:, :], in_=pb[si][:, 0:384], func=AF.Exp,
                                                            bias=cb_sb[:, qb * 128 + kb:qb * 128 + kb + 1], scale=SCALE),
                              [Rpb[si], Rtz], [RpT[pi_]])
                    else:
                        bi = c2["sb"] % 3
                        c2["sb"] += 1
                        sc.op("dve", lambda h: h.scalar_tensor_tensor(out=sbt[bi][:, :], in0=pb[si][:, 0:384], scalar=SCALE,
                                                                      in1=tz_sb[:, var, st:st + 384], op0=ALU.mult, op1=ALU.add),
                              [Rpb[si], Rtz], [Rsbt[bi]])
                        sc.op("act", lambda h: h.activation(out=pT[pi_][:, :], in_=sbt[bi][:, :], func=AF.Exp), [Rsbt[bi]], [RpT[pi_]])
                    return pi_

                def pv(kb, pi_):
                    for j in range(3):
                        sc.op("pe", lambda h, j=j: h.matmul(pb[j][:, 0:257], lhsT=pT[pi_][:, j * 128:(j + 1) * 128], rhs=vt[:, kb, :],
                                                           start=(kb == 0), stop=(kb == 127)),
                              [RpT[pi_], Rvt], [Rpb[j]], defer=(j < 2))

                pend = []
                for kb in range(128):
                    pend.append((kb, qk(kb)))
                    if len(pend) > 2:
                        k0, s0 = pend.pop(0)
                        pv(k0, expo(k0, s0))
                while pend:
                    k0, s0 = pend.pop(0)
                    pv(k0, expo(k0, s0))
                for j in range(3):
                    sc.op("dve", lambda h, j=j: h.reciprocal(out=sm2[:, j:j + 1], in_=pb[j][:, 256:257]), [Rpb[j]], [Rs2[j]])
                    sc.op("dve", lambda h, j=j: h.tensor_scalar(out=Amap[m][:, j, :], in0=pb[j][:, 0:256], scalar1=sm2[:, j:j + 1], scalar2=None, op0=ALU.mult),
                          [Rpb[j], Rs2[j]], [RA[m][j]])
            for j in range(3):
                et = qb * 3 + j
                sc.op("dve", lambda h, j=j: h.scalar_tensor_tensor(out=ot[:, :], in0=Amap[1][:, j, :], scalar=neglam[:, 0:1], in1=Amap[0][:, j, :],
                                                                  op0=ALU.mult, op1=ALU.add), [RA[0][j], RA[1][j], Rc], [Rot])
                sc.op("act", lambda h: h.activation(out=junk2[:, :], in_=ot[:, :], func=AF.Square, accum_out=sm2[:, 4:5]), [Rot], [Rj2, Rs2[4]])
                sc.op("dve", lambda h: h.tensor_scalar(out=sm2[:, 5:6], in0=sm2[:, 4:5], scalar1=1.0 / 256, scalar2=1e-5, op0=ALU.mult, op1=ALU.add), [Rs2[4]], [Rs2[5]])
                sc.op("act", lambda h: h.activation(out=sm2[:, 6:7], in_=sm2[:, 5:6], func=AF.Sqrt), [Rs2[5]], [Rs2[6]])
                sc.op("dve", lambda h: h.reciprocal(out=sm2[:, 7:8], in_=sm2[:, 6:7]), [Rs2[6]], [Rs2[7]])
                sc.op("dve", lambda h: h.scalar_tensor_tensor(out=onb[:, :], in0=ot[:, :], scalar=sm2[:, 7:8], in1=subg8[:, :], op0=ALU.mult, op1=ALU.mult),
                      [Rot, Rs2[7], Rc], [Ronb])
                for c_ in range(2):
                    sc.op("pe", lambda h, c_=c_: h.transpose(out=tp[:, c_ * 128:(c_ + 1) * 128], in_=onb[:, c_ * 128:(c_ + 1) * 128], identity=ident[:, :]),
                          [Ronb, Rc], [Rtp], defer=(c_ < 1))
                oi = c2["o"] % 2
                c2["o"] += 1
                sc.op("act", lambda h: h.copy(out=ost[oi][:, :, :], in_=tp[:, 0:256].rearrange("p (c t) -> p c t", c=2)), [Rtp], [Rost[oi]])
                sc.dma("pool", odT[2 * hh:2 * hh + 2, :, et * 128:(et + 1) * 128].rearrange("c p t -> p c t"), ost[oi][:, :, :], [Rost[oi]], [rs(("od", et))], ds_st)
    sc.barrier()
    es.close()

    es = ExitStack()
    KlTb = [A(f"KlT{i}", [128, DKN], BF16) for i in range(2)]
    QlTb = [A(f"QlT{i}", [128, EXT], BF16) for i in range(2)]
    RKlb, RQlb = [R(), R()], [R(), R()]
    bm_sbb = [A(f"bm_sb{i}", [128, 3, 2, 128], F32) for i in range(2)]
    kv_sb = A("kv_sb", [128, NCH], F32)
    Rbmb, Rkv = [R(), R()], R()
    vchb = [A(f"vch{i}", [128, NCH, 128], BF16) for i in range(2)]
    Rvchb = [R(), R()]
    accU = A("accU", [128, EXT], F32)
    accD = A("accD", [128, EXT], F32)
    RaU, RaD = R(), R()
    sq = A("sq", [128, EXT], F32)
    Rsq = R()
    rst = A("rst", [128, EXT], F32)
    Rrst = R()
    sb3 = [A(f"sb3{i}", [128, 2, 128], F32) for i in range(3)]
    Rsb3 = [R() for _ in range(3)]
    pT3 = [A(f"pT3{i}", [128, 2, 128], BF16) for i in range(3)]
    RpT3 = [R() for _ in range(3)]
    olb = [A(f"olb{i}", [128, 512], BF16) for i in range(2)]
    Rolb = [R(), R()]
    sc.dma("sp", kv_sb[:, :], kvi[:, :], [], [Rkv], ds_ld)
    c3 = {"b": 0, "o": 0}

    def dil_loads(hl):
        hb = hl % 2
        sc.dma("sp", KlTb[hb][:, :], lk[hl], [rs(("lk", hl))], [RKlb[hb]], ds_ld)
        sc.dma("sp", QlTb[hb][:, :], ql[hl], [rs(("ql", hl))], [RQlb[hb]], ds_ld)
        sc.dma("sp", bm_sbb[hb][:, :, :, :].rearrange("p a b c -> p (a b c)"), bmi[hl], [], [Rbmb[hb]], ds_ld)
        for p_, (_, d) in enumerate(PATS):
            nq = EXT // d
            nblk = -(-nq // 128)
            nch = nblk + 1
            for c in range(d):
                g0 = base[(p_, c)]
                row0 = c - 64 * d + 1024
                src = lv[hl, row0:row0 + d * 128 * nch:d, :].rearrange("(i j) c -> j i c", j=128)
                sc.dma("sp", vchb[hb][:, g0:g0 + nch, :], src, [rs(("lv", hl // 4))], [Rvchb[hb]], ds_ld)

    dil_loads(0)
    for hl in range(8):
        hb = hl % 2
        KlT, QlT, RKl, RQl = KlTb[hb], QlTb[hb], RKlb[hb], RQlb[hb]
        bm_sb, Rbm, vch, Rvch = bm_sbb[hb], Rbmb[hb], vchb[hb], Rvchb[hb]
        if hl + 1 < 8:
            dil_loads(hl + 1)
        sc.op("pool", lambda h: h.memset(accU[:, :], 0.0), [], [RaU])
        sc.op("pool", lambda h: h.memset(accD[:, :], 0.0), [], [RaD])
        blocks = []
        for p_, (_, d) in enumerate(PATS):
            nq = EXT // d
            nblk = -(-nq // 128)
            for c in range(d):
                for blk in range(nblk):
                    blocks.append((p_, d, c, blk, nq))

        def stage1(B):
            p_, d, c, blk, nq = B
            u0 = blk * 128
            n = min(128, nq - u0)
            kA = c + d * (u0 - 64) + 1024
            kB = kA + 128 * d
            q0 = c + d * u0
            qs = QlT[:, q0:q0 + d * (n - 1) + 1:d]
            bi = c3["b"] % 3
            c3["b"] += 1
            ps_s = pb[3 + bi]
            sc.op("pe", lambda h: h.matmul(ps_s[:, 0:n], lhsT=KlT[:, kA:kA + d * 127 + 1:d], rhs=qs, start=True, stop=True),
                  [RKl, RQl], [Rpb[3 + bi]], defer=True)
            sc.op("pe", lambda h: h.matmul(ps_s[0:n, 128:128 + n], lhsT=KlT[:, kB:kB + d * (n - 1) + 1:d], rhs=qs, start=True, stop=True),
                  [RKl, RQl], [Rpb[3 + bi]])
            return (bi, n, q0)

        def stage2(B, st):
            p_, d, c, blk, nq = B
            bi, n, q0 = st
            ps_s = pb[3 + bi]
            ps_o = pb[bi]
            gi = base[(p_, c)] + blk
            sc.op("dve", lambda h: h.scalar_tensor_tensor(out=sb3[bi][:, 0, 0:n], in0=ps_s[:, 0:n], scalar=SCALE, in1=bm_sb[:, p_, 0, 0:n],
                                                          op0=ALU.mult, op1=ALU.add), [Rpb[3 + bi], Rbm], [Rsb3[bi]])
            sc.op("dve", lambda h: h.scalar_tensor_tensor(out=sb3[bi][0:n, 1, 0:n], in0=ps_s[0:n, 128:128 + n], scalar=SCALE, in1=bm_sb[0:n, p_, 1, 0:n],
                                                          op0=ALU.mult, op1=ALU.add), [Rpb[3 + bi], Rbm], [Rsb3[bi]])
            sc.op("act", lambda h: h.activation(out=pT3[bi][:, 0, 0:n], in_=sb3[bi][:, 0, 0:n], func=AF.Exp, bias=kv_sb[:, gi:gi + 1]),
                  [Rsb3[bi], Rkv], [RpT3[bi]])
            sc.op("act", lambda h: h.activation(out=pT3[bi][0:n, 1, 0:n], in_=sb3[bi][0:n, 1, 0:n], func=AF.Exp, bias=kv_sb[0:n, gi + 1:gi + 2]),
                  [Rsb3[bi], Rkv], [RpT3[bi]])

        def stage3(B, st):
            p_, d, c, blk, nq = B
            bi, n, q0 = st
            ps_o = pb[bi]
            gi = base[(p_, c)] + blk
            sc.op("pe", lambda h: h.matmul(ps_o[:, 0:n], lhsT=vch[:, gi, :], rhs=pT3[bi][:, 0, 0:n], start=True, stop=False),
                  [Rvch, RpT3[bi]], [Rpb[bi]], defer=True)
            sc.op("pe", lambda h: h.matmul(ps_o[:, 0:n], lhsT=vch[0:n, gi + 1, :], rhs=pT3[bi][0:n, 1, 0:n], start=False, stop=True),
                  [Rvch, RpT3[bi]], [Rpb[bi]], defer=True)
            sc.op("pe", lambda h: h.matmul(ps_o[:, 128:128 + n], lhsT=ones_bf[:, :], rhs=pT3[bi][:, 0, 0:n], start=True, stop=False),
                  [Rc, RpT3[bi]], [Rpb[bi]], defer=True)
            sc.op("pe", lambda h: h.matmul(ps_o[:, 128:128 + n], lhsT=ones_bf[0:n, :], rhs=pT3[bi][0:n, 1, 0:n], start=False, stop=True),
                  [Rc, RpT3[bi]], [Rpb[bi]])
            esl = slice(q0, q0 + d * (n - 1) + 1, d)
            sc.op("dve", lambda h: h.tensor_tensor(out=accU[:, esl], in0=accU[:, esl], in1=ps_o[:, 0:n], op=ALU.add), [Rpb[bi], RaU], [RaU])
            sc.op("dve", lambda h: h.tensor_tensor(out=accD[:, esl], in0=accD[:, esl], in1=ps_o[:, 128:128 + n], op=ALU.add), [Rpb[bi], RaD], [RaD])

        nb_ = len(blocks)
        sts = {}
        for i in range(nb_ + 2):
            if i < nb_:
                sts[i] = stage1(blocks[i])
            if 0 <= i - 1 < nb_:
                stage2(blocks[i - 1], sts[i - 1])
            if 0 <= i - 2 < nb_:
                stage3(blocks[i - 2], sts[i - 2])
        sc.op("dve", lambda h: h.tensor_scalar(out=accD[:, :], in0=accD[:, :], scalar1=1e-18, scalar2=None, op0=ALU.max), [RaD], [RaD])
        sc.op("act", lambda h: h.activation(out=accD[:, :], in_=accD[:, :], func=AF.Ln), [RaD], [RaD])
        sc.op("act", lambda h: h.activation(out=accD[:, :], in_=accD[:, :], func=AF.Exp, scale=-1.0), [RaD], [RaD])
        sc.op("dve", lambda h: h.tensor_tensor(out=accU[:, :], in0=accU[:, :], in1=accD[:, :], op=ALU.mult), [RaU, RaD], [RaU])
        sc.op("pool", lambda h: h.tensor_tensor(out=sq[:, :], in0=accU[:, :], in1=accU[:, :], op=ALU.mult), [RaU], [Rsq])
        for c0 in range(0, EXT, 512):
            n = min(512, EXT - c0)
            pi = 3 + (c3["o"] % 3)
            sc.op("pe", lambda h: h.matmul(pb[pi][:, 0:n], lhsT=ones_f[:, :], rhs=sq[:, c0:c0 + n], start=True, stop=True), [Rc, Rsq], [Rpb[pi]])
            sc.op("dve", lambda h: h.tensor_scalar(out=rst[:, c0:c0 + n], in0=pb[pi][:, 0:n], scalar1=1.0 / 128, scalar2=1e-6, op0=ALU.mult, op1=ALU.add),
                  [Rpb[pi]], [Rrst])
            sc.op("act", lambda h: h.activation(out=rst[:, c0:c0 + n], in_=rst[:, c0:c0 + n], func=AF.Ln), [Rrst], [Rrst])
            sc.op("act", lambda h: h.activation(out=rst[:, c0:c0 + n], in_=rst[:, c0:c0 + n], func=AF.Exp, scale=-0.5), [Rrst], [Rrst])
            oi = c3["o"] % 2
            c3["o"] += 1
            sc.op("dve", lambda h: h.scalar_tensor_tensor(out=olb[oi][:, 0:n], in0=accU[:, c0:c0 + n], scalar=misc_sb[:, O_DILG + hl:O_DILG + hl + 1],
                                                          in1=rst[:, c0:c0 + n], op0=ALU.mult, op1=ALU.mult), [RaU, Rrst, Rc], [Rolb[oi]])
            sc.dma("pool", odT[8 + hl, :, c0:c0 + n], olb[oi][:, 0:n], [Rolb[oi]], [rs(("ol", hl))], ds_st)
    sc.barrier()
    es.close()

    es = ExitStack()
    Wo = A("Wo", [128, 16, DM], BF16)
    RWo = R()
    g2 = A("g2", [128, DM], F32)
    gf = A("gf", [128, DM], F32)
    Rg2 = R()
    sc.dma("sp", Wo[:, :, :].rearrange("p k c -> p (k c)"), wout_b[:, :], [Rwout[0]], [RWo], ds_ld)
    sc.dma("sp", g2[:, :], gains[1], [], [Rg2], ds_c)
    sc.dma("sp", gf[:, :], gains[2], [], [Rg2], ds_c)
    Rh2 = R()
    sc.dma("pool", h2s[:, :, 0:1].rearrange("k p o -> p k o"), zeros_bf[:, 0:16].rearrange("p (k o) -> p k o", o=1), [Rc], [Rh2], ds_st, slow=True)
    sc.dma("pool", h2s[:, :, EXT + 1:EXT + 2].rearrange("k p o -> p k o"), zeros_bf[:, 0:16].rearrange("p (k o) -> p k o", o=1), [Rc], [Rh2], ds_st, slow=True)
    OTb = [A(f"OT{i}", [128, 16, 128], BF16) for i in range(2)]
    ROT = [R(), R()]
    xt4 = [A(f"xt4{i}", [128, DM], F32) for i in range(2)]
    Rxt4 = [R(), R()]
    x1b = [A(f"x1b{i}", [128, DM], F32) for i in range(2)]
    Rx1 = [R(), R()]
    h2b = [A(f"h2b{i}", [128, DM], BF16) for i in range(2)]
    Rh2b = [R(), R()]
    hst = [A(f"hst{i}", [128, 16, 128], BF16) for i in range(2)]
    Rhst = [R(), R()]
    junk4 = A("junk4", [128, DM], BF16)
    ssb4 = A("ssb4", [128, 4], F32)
    junk, Rjunk, ssb = junk4, R(), ssb4
    Rss[0], Rss[1], Rrs[0], Rrs[1] = R(), R(), R(), R()
    Rx1s = [R() for _ in range(NET)]
    for et in range(NET):
        i2 = et % 2
        r0 = (S - 128) if et == 0 else (et - 1) * 128
        od_deps = [rs(("od", et))] + [rs(("ol", h_)) for h_ in range(8)]
        sc.dma("sp", OTb[i2][:, :, :], odT[:, :, et * 128:(et + 1) * 128].rearrange("c p t -> p c t"), od_deps, [ROT[i2]], ds_ld)
        sc.dma("sp", xt4[i2][:, :], xr[r0:r0 + 128, :], [], [Rxt4[i2]], ds_x)
        for db in range(4):
            for ci in range(16):
                sc.op("pe", lambda h, ci=ci: h.matmul(pb[db][:, :], lhsT=OTb[i2][:, ci, :], rhs=Wo[:, ci, db * 512:(db + 1) * 512], start=(ci == 0), stop=(ci == 15)),
                      [ROT[i2], RWo], [Rpb[db]], defer=(ci < 15))
            sc.op("dve", lambda h, db=db: h.tensor_tensor(out=x1b[i2][:, db * 512:(db + 1) * 512], in0=pb[db][:, :], in1=xt4[i2][:, db * 512:(db + 1) * 512], op=ALU.add),
                  [Rpb[db], Rxt4[i2]], [Rx1[i2]])
        sc.dma("pool", x1s[et * 128:(et + 1) * 128, :], x1b[i2][:, :], [Rx1[i2]], [Rx1s[et]], ds_st)
        norm_tile(None, x1b[i2], Rx1[i2], g2, Rg2, h2b[i2], Rh2b[i2], i2, extra_scale=misc_sb[:, O_TM + et:O_TM + et + 1])
        transpose16(h2b[i2], Rh2b[i2], hst[i2][:, :, :], [Rhst[i2]])
        sc.dma("pool", h2s[:, :, 1 + et * 128:1 + (et + 1) * 128].rearrange("k p t -> p k t"), hst[i2][:, :, :], [Rhst[i2]], [Rh2], ds_st)
    sc.barrier()
    es.close()

    es = ExitStack()
    gf = A("gfb", [128, DM], F32)
    Rgf = R()
    sc.dma("sp", gf[:, :], gains[2], [], [Rgf], ds_c)
    h2g = [A(f"h2g{i}", [128, 16, 386], BF16) for i in range(2)]
    Rh2g = [R(), R()]
    x1g = [A(f"x1g{i}", [128, 3, DM], F32) for i in range(1)] * 2
    Rx1g = [[R() for _ in range(3)]] * 2
    wgt = [A(f"wgt{i}", [128, 16, 128], BF16) for i in range(3)]
    wut = [A(f"wut{i}", [128, 16, 128], BF16) for i in range(3)]
    Rwgt = [R() for _ in range(3)]
    Rwut = [R() for _ in range(3)]
    aT = A("aT", [128, NFC, 384], BF16)
    RaT = [R() for _ in range(NFC)]
    wds = [A(f"wds{i}", [128, NFC, 256], BF16) for i in range(2)]
    Rwds = [R(), R()]
    ca = [A(f"ca{i}", [128, 384], F32) for i in range(4)]
    Rca = [R() for _ in range(4)]
    yo = [A(f"yo{i}", [128, DM], F32) for i in range(2)]
    Ryo = [R(), R()]
    junk5 = A("junk5", [128, DM], BF16)
    Rj5 = R()
    sm5 = A("sm5", [128, 8], F32)
    Rs5 = [R() for _ in range(4)]
    Ry = R()
    c4 = {"w": 0, "d": 0, "y": 0}
    CW, CB = O_CW, O_CB
    for g in range(6):
        gi2 = g % 2
        sc.dma("sp", h2g[gi2][:, :, :], h2s[:, :, g * 384:g * 384 + 386].rearrange("k p t -> p k t"), [Rh2], [Rh2g[gi2]], ds_ld)
        for tt in range(3):
            et = g * 3 + tt
            sc.dma("sp", x1g[gi2][:, tt, :], x1s[et * 128:(et + 1) * 128, :], [Rx1s[et]], [Rx1g[gi2][tt]], ds_x)
        for fc in range(NFC):
            wi = c4["w"] % 3
            c4["w"] += 1
            sc.dma("sp", wgt[wi][:, :, :].rearrange("p k c -> p (k c)"), wgu_b[fc], [Rwgu[fc]], [Rwgt[wi]], ds_ld)
            sc.dma("sp", wut[wi][:, :, :].rearrange("p k c -> p (k c)"), wgu_b[NFC + fc], [Rwgu[NFC + fc]], [Rwut[wi]], ds_ld)
            pg = fc % 2
            pu = 2 + fc % 2
            for k in range(16):
                sc.op("pe", lambda h, k=k: h.matmul(pb[pg][:, 0:386], lhsT=wgt[wi][:, k, :], rhs=h2g[gi2][:, k, 0:386], start=(k == 0), stop=(k == 15)),
                      [Rwgt[wi], Rh2g[gi2]], [Rpb[pg]], defer=(k < 15))
            for k in range(16):
                sc.op("pe", lambda h, k=k: h.matmul(pb[pu][:, 0:384], lhsT=wut[wi][:, k, :], rhs=h2g[gi2][:, k, 1:385], start=(k == 0), stop=(k == 15)),
                      [Rwut[wi], Rh2g[gi2]], [Rpb[pu]], defer=(k < 15))
            a0, a1, a2_, a3 = [(4 * 0 + j) for j in range(4)]
            cw0 = misc_sb[:, CW + fc * 3 + 0:CW + fc * 3 + 1]
            cw1 = misc_sb[:, CW + fc * 3 + 1:CW + fc * 3 + 2]
            cw2 = misc_sb[:, CW + fc * 3 + 2:CW + fc * 3 + 3]
            cbb = misc_sb[:, CB + fc:CB + fc + 1]
            sc.op("act", lambda h: h.activation(out=ca[0][:, :], in_=pb[pg][:, 1:385], func=AF.Identity, bias=cbb, scale=cw1), [Rpb[pg], Rc], [Rca[0]])
            sc.op("dve", lambda h: h.scalar_tensor_tensor(out=ca[1][:, :], in0=pb[pg][:, 0:384], scalar=cw0, in1=ca[0][:, :], op0=ALU.mult, op1=ALU.add),
                  [Rpb[pg], Rca[0], Rc], [Rca[1]])
            sc.op("dve", lambda h: h.scalar_tensor_tensor(out=ca[2][:, :], in0=pb[pg][:, 2:386], scalar=cw2, in1=ca[1][:, :], op0=ALU.mult, op1=ALU.add),
                  [Rpb[pg], Rca[1], Rc], [Rca[2]])
            sc.op("act", lambda h: h.activation(out=ca[3][:, :], in_=ca[2][:, :], func=AF.Silu), [Rca[2]], [Rca[3]])
            sc.op("dve", lambda h: h.tensor_tensor(out=aT[:, fc, :], in0=ca[3][:, :], in1=pb[pu][:, 0:384], op=ALU.mult), [Rca[3], Rpb[pu]], [RaT[fc]])
        for db8 in range(8):
            di = c4["d"] % 2
            c4["d"] += 1
            sc.dma("sp", wds[di][:, :, :].rearrange("p f c -> p (f c)"), wdn_b[db8], [Rwdn[db8]], [Rwds[di]], ds_ld)
            for tt in range(3):
                pi = 4 + (tt + db8) % 2
                for fc in range(NFC):
                    sc.op("pe", lambda h, fc=fc: h.matmul(pb[pi][:, 0:256], lhsT=aT[:, fc, tt * 128:(tt + 1) * 128], rhs=wds[di][:, fc, :], start=(fc == 0), stop=(fc == NFC - 1)),
                          [RaT[fc], Rwds[di]], [Rpb[pi]], defer=(fc < NFC - 1))
                xs_ = x1g[gi2][:, tt, db8 * 256:(db8 + 1) * 256]
                sc.op("dve", lambda h: h.tensor_tensor(out=xs_, in0=pb[pi][:, 0:256], in1=xs_, op=ALU.add), [Rpb[pi], Rx1g[gi2][tt]], [Rx1g[gi2][tt]])
        for tt in range(3):
            et = g * 3 + tt
            if et < 1 or et > 16:
                continue
            yi = c4["y"] % 2
            c4["y"] += 1
            xs_ = x1g[gi2][:, tt, :]
            sc.op("act", lambda h: h.activation(out=junk5[:, :], in_=xs_, func=AF.Square, accum_out=sm5[:, 0:1]), [Rx1g[gi2][tt]], [Rj5, Rs5[0]])
            sc.op("dve", lambda h: h.tensor_scalar(out=sm5[:, 1:2], in0=sm5[:, 0:1], scalar1=1.0 / DM, scalar2=1e-6, op0=ALU.mult, op1=ALU.add), [Rs5[0]], [Rs5[1]])
            sc.op("act", lambda h: h.activation(out=sm5[:, 2:3], in_=sm5[:, 1:2], func=AF.Sqrt), [Rs5[1]], [Rs5[2]])
            sc.op("dve", lambda h: h.reciprocal(out=sm5[:, 3:4], in_=sm5[:, 2:3]), [Rs5[2]], [Rs5[3]])
            sc.op("dve", lambda h: h.scalar_tensor_tensor(out=yo[yi][:, :], in0=xs_, scalar=sm5[:, 3:4], in1=gf[:, :], op0=ALU.mult, op1=ALU.mult),
                  [Rx1g[gi2][tt], Rs5[3], Rgf], [Ryo[yi]])
            sc.dma("pool", y[(et - 1) * 128:et * 128, :], yo[yi][:, :], [Ryo[yi]], [Ry], ds_st)
    sc.barrier()
    es.close()
    return nc


def kernel(x, norm1_gain, w_in, rel_bias_table, lambda_q1, lambda_k1, lambda_q2, lambda_k2,
           diff_subln_gain, dil_out_gain, w_out, norm2_gain, w_gate_up, conv_w, conv_b,
           w_down, final_gain):
    f = np.float32
    x2 = np.asarray(x, f)[0]
    table = np.asarray(rel_bias_table, f)
    w_in0 = np.asarray(w_in, f)[0]
    blocks = np.ascontiguousarray(w_in0.reshape(16, 128, 48, 128).transpose(2, 1, 0, 3)).reshape(48, 128, 2048)
    wink = np.ascontiguousarray(np.concatenate([blocks[0:8], blocks[8:16], blocks[24:32], blocks[32:40]], 0))

    def vblk(cols):
        return np.ascontiguousarray(cols.reshape(16, 128, 2, 512).transpose(2, 1, 0, 3)).reshape(2, 128, 16 * 512)

    winv = vblk(w_in0[:, 2048:3072])
    winlv = vblk(w_in0[:, 5120:6144])
    wout_h = np.ascontiguousarray(np.asarray(w_out, f)[0].reshape(16, 128, DM).transpose(1, 0, 2)).reshape(128, 16 * DM)
    wgu_h = np.ascontiguousarray(np.asarray(w_gate_up, f)[0].reshape(16, 128, 88, 128).transpose(2, 1, 0, 3)).reshape(88, 128, 2048)
    wdn_h = np.ascontiguousarray(np.asarray(w_down, f)[0].reshape(NFC, 128, 8, 256).transpose(2, 1, 0, 3)).reshape(8, 128, NFC * 256)
    gains = np.ascontiguousarray(np.stack([
        np.broadcast_to(np.asarray(norm1_gain, f)[0], (128, DM)),
        np.broadcast_to(np.asarray(norm2_gain, f)[0], (128, DM)),
        np.broadcast_to(np.asarray(final_gain, f), (128, DM))], 0))
    MISC = 512 + 256 + 8 + NFC * 3 + NFC + NET
    ident = np.eye(128, dtype=f)
    in_maps = []
    for c in range(NCORE):
        tz, cbv, bm, kv = _host_tables(c, table)
        misc = np.zeros((128, MISC), f)
        misc[:, 0:128] = np.asarray(lambda_q1, f)[0][None]
        misc[:, 128:256] = np.asarray(lambda_k1, f)[0][None]
        misc[:, 256:384] = np.asarray(lambda_q2, f)[0][None]
        misc[:, 384:512] = np.asarray(lambda_k2, f)[0][None]
        misc[:, 512:768] = np.asarray(diff_subln_gain, f)[0][None]
        misc[:, 768:776] = np.asarray(dil_out_gain, f)[0].reshape(8, 128).T
        misc[:, 776:776 + NFC * 3] = np.asarray(conv_w, f)[0].T.reshape(NFC, 128, 3).transpose(1, 0, 2).reshape(128, NFC * 3)
        misc[:, 776 + NFC * 3:776 + NFC * 4] = np.asarray(conv_b, f)[0].reshape(NFC, 128).T
        tm = np.ones(NET, f)
        if c == 0:
            tm[0] = 0.0
        if c == NCORE - 1:
            tm[NET - 1] = 0.0
        misc[:, 776 + NFC * 4:] = tm[None]
        in_maps.append({
            "xr": np.ascontiguousarray(np.roll(x2, -OWN * c, axis=0)),
            "gains": gains, "wink": wink, "winv": winv, "winlv": winlv, "wout": wout_h,
            "wgu": wgu_h, "wdn": wdn_h,
            "tz": np.ascontiguousarray(tz.reshape(4, 128, 3 * TZW)), "cbv": cbv,
            "bm": np.ascontiguousarray(bm.reshape(8, 128, 768)), "kv": kv,
            "misc": misc, "ident": ident,
        })
    nc = build_nc()
    res = run_bass_kernel_spmd(nc, in_maps, core_ids=list(range(NCORE)))
    out = np.concatenate([np.asarray(res.results[c]["y"], f) for c in range(NCORE)], 0)
    return out[None].astype(np.float32)
```

```python
import math
import os
from contextlib import ExitStack
import numpy as np
import ml_dtypes
import concourse.bass as bass
import concourse.mybir as mybir
from concourse.bass_utils import run_bass_kernel_spmd

F32 = mybir.dt.float32
BF16 = mybir.dt.bfloat16
AF = mybir.ActivationFunctionType
ALU = mybir.AluOpType
AX = mybir.AxisListType

S = 16384
DM = 2048
NCORE = 8
OWN = 2048
EXT = 2304
NET = 18
SCALE = 1.0 / math.sqrt(128.0)
J0 = 942
TZW = 2048
DKN = 4352
NFC = 44
NEG = -1e30
PATS = ((128, 1), (512, 4), (2048, 16))


class Res:
    __slots__ = ("w", "rs", "pend")

    def __init__(self):
        self.w = None
        self.rs = {}
        self.pend = None


class _Eng:
    pass


class _DSem:
    pass


class Sch:
    LIMIT = 30000

    def __init__(self, nc):
        self.nc = nc
        self.E = {}
        self.nsem = 0
        self.dsems = []
        for n, h in (("pe", nc.tensor), ("act", nc.scalar), ("dve", nc.vector),
                     ("pool", nc.gpsimd), ("sp", nc.sync)):
            e = _Eng()
            e.h = h
            e.name = n
            e.sem = self._newsem("e_" + n)
            e.cnt = 0
            e.waited = {}
            e.deferred = []
            self.E[n] = e

    def _newsem(self, name):
        self.nsem += 1
        return self.nc.alloc_semaphore(f"{name}_{self.nsem}")

    def dsem(self):
        d = _DSem()
        d.h = self._newsem("d")
        d.cnt = 0
        self.dsems.append(d)
        return d

    def _wait(self, e, tok):
        if tok is None:
            return
        sem, val = tok
        k = id(sem)
        if e.waited.get(k, (None, 0))[1] >= val:
            return
        e.h.wait_ge(sem, val)
        e.waited[k] = (sem, val)

    def _gather(self, e, reads, writes):
        for r in reads:
            assert r.pend is None or r.pend is e, "read of pending resource"
            self._wait(e, r.w)
        for w in writes:
            assert w.pend is None or w.pend is e, "write of pending resource"
            self._wait(e, w.w)
            for t in w.rs.values():
                self._wait(e, t)

    @staticmethod
    def _commit(tok, reads, writes):
        for r in reads:
            r.rs[id(tok[0])] = tok
            r.pend = None
        for w in writes:
            w.w = tok
            w.rs = {}
            w.pend = None

    def op(self, eng, fn, reads=(), writes=(), defer=False):
        e = self.E[eng]
        self._gather(e, reads, writes)
        ins = fn(e.h)
        if defer:
            e.deferred.append((tuple(reads), tuple(writes)))
            for r in reads:
                r.pend = e
            for w in writes:
                w.pend = e
            return
        if e.cnt >= self.LIMIT:
            e.sem = self._newsem("e_" + e.name)
            e.cnt = 0
        e.cnt += 1
        ins.then_inc(e.sem, 1)
        tok = (e.sem, e.cnt)
        for rr, ww in e.deferred:
            self._commit(tok, rr, ww)
        e.deferred = []
        self._commit(tok, reads, writes)

    def dma(self, q, out, in_, reads, writes, sem, slow=False):
        e = self.E[q]
        self._gather(e, reads, writes)
        if sem.cnt + 16 > self.LIMIT:
            sem.h = self._newsem("d")
            sem.cnt = 0
        ins = e.h.dma_start(out=out, in_=in_, allow_slow_non_contiguous=True) if slow else e.h.dma_start(out=out, in_=in_)
        sem.cnt += 16
        ins.then_inc(sem.h, 16)
        self._commit((sem.h, sem.cnt), reads, writes)

    def barrier(self):
        toks = []
        for e in self.E.values():
            assert not e.deferred
            if e.cnt > 0:
                toks.append((e.sem, e.cnt))
        for d in self.dsems:
            if d.cnt > 0:
                toks.append((d.h, d.cnt))
        for e in self.E.values():
            for t in toks:
                self._wait(e, t)


def _bucket(rel):
    rel = np.asarray(rel, dtype=np.int64)
    ret = np.where(rel > 0, 16, 0)
    n = np.abs(rel)
    nf = np.maximum(n, 1).astype(np.float32)
    large = 8 + (np.log(nf / np.float32(8)) / np.float32(math.log(128.0)) * np.float32(8)).astype(np.int32)
    large = np.minimum(large, 15)
    return ret + np.where(n < 8, n, large)


def _diff_tile_kind(kb, qb):
    D = 128 * kb - (384 * qb - 128)
    if kb >= 120 and (D - S) > -686:
        return 1, J0 - (D - S)
    if -686 < D < 942:
        return (0 if kb < 16 else 2), J0 - D
    return None, None


def _dil_chunks():
    base = {}
    gi = 0
    for p, (_, d) in enumerate(PATS):
        nq = EXT // d
        nblk = -(-nq // 128)
        for c in range(d):
            base[(p, c)] = gi
            gi += nblk + 1
    return base, gi


def _host_tables(core, table):
    tdiff = table[:, :4]
    tdil = table[:, 4:]
    p = np.arange(128)[:, None]
    j = np.arange(TZW)[None, :]
    rho = p - j + J0
    bk = _bucket(rho)
    tz = np.zeros((4, 128, 3, TZW), np.float32)
    for h in range(4):
        f = tdiff[bk, h]
        tz[h, :, 0] = f
        tz[h, :, 1] = f if core >= 1 else tdiff[31, h]
        tz[h, :, 2] = f if core < 7 else tdiff[15, h]
    cbv = np.zeros((4, 128, 6 * 128), np.float32)
    for qb in range(6):
        tq = (384 * qb - 128 + 192) + 2048 * core
        for kb in range(128):
            tk = (128 * kb + 64 + 2048 * core) % S
            b = 31 if tk > tq else 15
            cbv[:, :, qb * 128 + kb] = tdiff[b, :][:, None]
    bm = np.full((8, 128, 3, 2, 128), NEG, np.float32)
    lane = np.arange(128)[:, None]
    q = np.arange(128)[None, :]
    for pi, (_, d) in enumerate(PATS):
        for ch in range(2):
            off = lane - 64 - q if ch == 0 else lane + 64 - q
            ok = np.abs(off) <= 64
            bkt = _bucket(off * d)
            for h in range(8):
                bm[h, :, pi, ch, :] = np.where(ok, tdil[bkt, h], NEG)
    base, ntot = _dil_chunks()
    kv = np.zeros((128, ntot), np.float32)
    for pi, (_, d) in enumerate(PATS):
        nq = EXT // d
        nblk = -(-nq // 128)
        for c in range(d):
            for i in range(nblk + 1):
                dk = c + d * (128 * i - 64 + np.arange(128)) + 1024
                t = dk - 1152 + 2048 * core
                kv[:, base[(pi, c)] + i] = np.where((t >= 0) & (t < S), 0.0, NEG)
    return tz, cbv, bm, kv


def build_nc():
    nc = bass.Bass("TRN2", target_bir_lowering=False)
    sc = Sch(nc)

    def din(name, shape, dt=F32):
        return nc.dram_tensor(name, list(shape), dt, kind="ExternalInput").ap()

    def dscr(name, shape, dt=BF16):
        return nc.dram_tensor(name, list(shape), dt).ap()

    base, NCH = _dil_chunks()
    MISC = 512 + 256 + 8 + NFC * 3 + NFC + NET
    xr = din("xr", [S, DM])
    gains = din("gains", [3, 128, DM])
    wink = din("wink", [32, 128, 16 * 128])
    winv = din("winv", [2, 128, 16 * 512])
    winlv = din("winlv", [2, 128, 16 * 512])
    wout = din("wout", [128, 16 * DM])
    wgu = din("wgu", [88, 128, 16 * 128])
    wdn = din("wdn", [8, 128, NFC * 256])
    tzi = din("tz", [4, 128, 3 * TZW])
    cbi = din("cbv", [4, 128, 768])
    bmi = din("bm", [8, 128, 3 * 2 * 128])
    kvi = din("kv", [128, NCH])
    misc = din("misc", [128, MISC])
    identi = din("ident", [128, 128])
    y = nc.dram_tensor("y", [OWN, DM], F32, kind="ExternalOutput").ap()

    wink_b = dscr("wink_b", [32, 128, 16 * 128])
    winv_b = dscr("winv_b", [2, 128, 16 * 512])
    winlv_b = dscr("winlv_b", [2, 128, 16 * 512])
    wout_b = dscr("wout_b", [128, 16 * DM])
    wgu_b = dscr("wgu_b", [88, 128, 16 * 128])
    wdn_b = dscr("wdn_b", [8, 128, NFC * 256])
    hk = dscr("hk", [8, 128, S])
    hv = dscr("hv", [4, S, 256])
    qd = dscr("qd", [8, 128, EXT])
    lk = dscr("lk", [8, 128, DKN])
    lv = dscr("lv", [8, DKN + 2048, 128])
    ql = dscr("ql", [8, 128, EXT])
    odT = dscr("odT", [16, 128, EXT])
    x1s = dscr("x1s", [EXT, DM], F32)
    h2s = dscr("h2s", [16, 128, EXT + 2])

    R = Res
    pctr = {"n": 0}

    def psum_alloc(es_, nfp, tpcols):
        pctr["n"] += 1
        p_ = [es_.enter_context(nc.psum_tensor(f"pb{pctr['n']}_{i}", [128, 512], F32)) for i in range(nfp)]
        t_ = es_.enter_context(nc.psum_tensor(f"tp{pctr['n']}", [128, tpcols], BF16)) if tpcols else None
        return p_, [R() for _ in range(nfp)], t_, R()

    ident = nc.alloc_sbuf_tensor("ident_sb", [128, 128], BF16)
    misc_sb = nc.alloc_sbuf_tensor("misc_sb", [128, MISC], F32)
    neglam = nc.alloc_sbuf_tensor("neglam", [128, 1], F32)
    subg8 = nc.alloc_sbuf_tensor("subg8", [128, 256], F32)
    ones_bf = nc.alloc_sbuf_tensor("ones_bf", [128, 128], BF16)
    ones_f = nc.alloc_sbuf_tensor("ones_f", [128, 128], F32)
    zeros_bf = nc.alloc_sbuf_tensor("zeros_bf", [128, 16], BF16)
    sm = nc.alloc_sbuf_tensor("sm", [128, 64], F32)
    Rc = R()
    Rsm = [R() for _ in range(8)]
    O_LAM, O_SUBG, O_DILG, O_CW, O_CB, O_TM = 0, 512, 768, 776, 776 + NFC * 3, 776 + NFC * 4

    ds_w = sc.dsem()
    ds_c = sc.dsem()
    sc.dma("pool", ident[:, :], identi[:, :], [], [Rc], ds_c)
    sc.dma("sp", misc_sb[:, :], misc[:, :], [], [Rc], ds_c)
    sc.op("pool", lambda h: h.memset(ones_bf[:, :], 1.0), [], [Rc])
    sc.op("pool", lambda h: h.memset(ones_f[:, :], 1.0), [], [Rc])
    sc.op("pool", lambda h: h.memset(zeros_bf[:, :], 0.0), [], [Rc])
    lt = nc.alloc_sbuf_tensor("lt", [128, 2, 128], F32)
    sc.op("dve", lambda h: h.tensor_tensor(out=lt[:, 0, :], in0=misc_sb[:, 0:128], in1=misc_sb[:, 128:256], op=ALU.mult), [Rc], [Rsm[0]])
    sc.op("dve", lambda h: h.tensor_tensor(out=lt[:, 1, :], in0=misc_sb[:, 256:384], in1=misc_sb[:, 384:512], op=ALU.mult), [Rc], [Rsm[1]])
    sc.op("dve", lambda h: h.reduce_sum(out=sm[:, 0:1], in_=lt[:, 0, :], axis=AX.X), [Rsm[0]], [Rsm[2]])
    sc.op("dve", lambda h: h.reduce_sum(out=sm[:, 1:2], in_=lt[:, 1, :], axis=AX.X), [Rsm[1]], [Rsm[3]])
    sc.op("act", lambda h: h.activation(out=sm[:, 2:3], in_=sm[:, 0:1], func=AF.Exp), [Rsm[2]], [Rsm[4]])
    sc.op("act", lambda h: h.activation(out=sm[:, 3:4], in_=sm[:, 1:2], func=AF.Exp), [Rsm[3]], [Rsm[5]])
    sc.op("dve", lambda h: h.tensor_tensor(out=sm[:, 4:5], in0=sm[:, 3:4], in1=sm[:, 2:3], op=ALU.subtract), [Rsm[4], Rsm[5]], [Rsm[6]])
    sc.op("dve", lambda h: h.tensor_scalar(out=neglam[:, :], in0=sm[:, 4:5], scalar1=-0.2, scalar2=None, op0=ALU.add), [Rsm[6]], [Rc])
    sc.op("dve", lambda h: h.tensor_scalar(out=subg8[:, :], in0=misc_sb[:, O_SUBG:O_SUBG + 256], scalar1=0.8, scalar2=None, op0=ALU.mult), [Rc], [Rc])

    def cast_w(dst, src, n, width, R_list):
        for b in range(n):
            for c0 in range(0, width, 2048):
                c1 = min(width, c0 + 2048)
                sc.dma("pool", dst[b, :, c0:c1], src[b, :, c0:c1], [], [R_list[b]], ds_w)

    Rwink = [R() for _ in range(32)]
    Rwinv = [R() for _ in range(2)]
    Rwinlv = [R() for _ in range(2)]
    Rwout = [R()]
    Rwgu = [R() for _ in range(88)]
    Rwdn = [R() for _ in range(8)]
    cast_w(winv_b, winv, 2, 16 * 512, Rwinv)
    cast_w(wink_b, wink, 32, 16 * 128, Rwink)
    cast_w(winlv_b, winlv, 2, 16 * 512, Rwinlv)

    def rstd_chain(ss_ap, out_ap, inv_n, eps, Rin, Rout, tmp):
        a, b = tmp
        sc.op("dve", lambda h: h.tensor_scalar(out=sm[:, a:a + 1], in0=ss_ap, scalar1=inv_n, scalar2=eps, op0=ALU.mult, op1=ALU.add), [Rin], [Rsm[6]])
        sc.op("act", lambda h: h.activation(out=sm[:, b:b + 1], in_=sm[:, a:a + 1], func=AF.Sqrt), [Rsm[6]], [Rsm[7]])
        sc.op("dve", lambda h: h.reciprocal(out=out_ap, in_=sm[:, b:b + 1]), [Rsm[7]], [Rout])

    ds_x = sc.dsem()
    ds_ld = sc.dsem()
    ds_st = sc.dsem()
    Rscr = {}

    def rs(key):
        if key not in Rscr:
            Rscr[key] = R()
        return Rscr[key]

    es = ExitStack()

    def A(name, shape, dt):
        return es.enter_context(nc.sbuf_tensor(name, shape, dt))

    pb, Rpb, tp, Rtp = psum_alloc(es, 6, 2048)
    P1 = {}
    P1["g1"] = A("g1", [128, DM], F32)
    Rg1 = R()
    sc.dma("sp", P1["g1"][:, :], gains[0], [], [Rg1], ds_c)
    hTb = [A(f"hT{i}", [128, 16, 1024], BF16) for i in range(2)]
    RhT = [[R() for _ in range(8)] for _ in range(2)]
    xtb = [A(f"xt{i}", [128, DM], F32) for i in range(2)]
    Rxt = [R(), R()]
    xnb = [A(f"xn{i}", [128, DM], BF16) for i in range(2)]
    Rxn = [R(), R()]
    junk = A("junk", [128, DM], BF16)
    Rjunk = R()
    ssb = A("ssb", [128, 4], F32)
    Rss = [R(), R()]
    Rrs = [R(), R()]
    wkt = [A(f"wkt{i}", [128, 16, 128], BF16) for i in range(3)]
    Rwkt = [R() for _ in range(3)]
    wvt = [A(f"wvt{i}", [128, 16, 512], BF16) for i in range(2)]
    Rwvt = [R() for _ in range(2)]
    stg = [A(f"stg{i}", [128, 512], BF16) for i in range(4)]
    Rstg = [R() for _ in range(4)]
    cnt = {"x": 0, "wk": 0, "wv": 0, "stg": 0, "pb": 0}

    def next_pb():
        i = cnt["pb"] % 6
        cnt["pb"] += 1
        return i

    def evac_store(psap, n_free_shape, dst_ap, Rp, Rdst, use_act):
        i = cnt["stg"] % 4
        cnt["stg"] += 1
        st_ap = n_free_shape(stg[i])
        if use_act:
            sc.op("act", lambda h: h.copy(out=st_ap, in_=psap), [Rp], [Rstg[i]])
        else:
            sc.op("dve", lambda h: h.tensor_copy(out=st_ap, in_=psap), [Rp], [Rstg[i]])
        sc.dma("pool", dst_ap, st_ap, [Rstg[i]], [Rdst], ds_st)

    def norm_tile(src_rows, xt, Rx, gain_t, Rg, xn, Rn, i2, extra_scale=None):
        sc.op("act", lambda h: h.activation(out=junk[:, :], in_=xt[:, :], func=AF.Square, accum_out=ssb[:, i2:i2 + 1]), [Rx], [Rjunk, Rss[i2]])
        rstd_chain(ssb[:, i2:i2 + 1], ssb[:, 2 + i2:3 + i2], 1.0 / DM, 1e-6, Rss[i2], Rrs[i2], (8, 9))
        if extra_scale is not None:
            sc.op("dve", lambda h: h.tensor_tensor(out=ssb[:, 2 + i2:3 + i2], in0=ssb[:, 2 + i2:3 + i2], in1=extra_scale, op=ALU.mult), [Rrs[i2], Rc], [Rrs[i2]])
        sc.op("dve", lambda h: h.scalar_tensor_tensor(out=xn[:, :], in0=xt[:, :], scalar=ssb[:, 2 + i2:3 + i2], in1=gain_t[:, :], op0=ALU.mult, op1=ALU.mult), [Rx, Rrs[i2], Rg], [Rn])

    def transpose16(xn, Rn, dst_ap, Rdst_list):
        for k in range(16):
            sc.op("pe", lambda h, k=k: h.transpose(out=tp[:, k * 128:(k + 1) * 128], in_=xn[:, k * 128:(k + 1) * 128], identity=ident[:, :]),
                  [Rn, Rc], [Rtp], defer=(k < 15))
        sc.op("act", lambda h: h.copy(out=dst_ap, in_=tp[:, :].rearrange("p (k c) -> p k c", k=16)), [Rtp], Rdst_list)

    def tile_groups(ta, tb):
        g = []
        t = ta
        while t < tb:
            e = min(tb, t + 4)
            g.append((t, e))
            t = e
        return g

    def dkcol(T):
        return (T - 119) * 128 if T >= 119 else (T + 9) * 128

    def ecol(T):
        return 0 if T == 127 else (T + 1) * 128

    def norm_items(ST):
        hb = ST % 2
        hT = hTb[hb]
        items = []
        for t in range(8):
            def f(t=t):
                T = ST * 8 + t
                i2 = cnt["x"] % 2
                cnt["x"] += 1
                sc.dma("sp", xtb[i2][:, :], xr[T * 128:(T + 1) * 128, :], [], [Rxt[i2]], ds_x)
                norm_tile(None, xtb[i2], Rxt[i2], P1["g1"], Rg1, xnb[i2], Rxn[i2], i2)
                transpose16(xnb[i2], Rxn[i2], hT[:, :, t * 128:(t + 1) * 128], [RhT[hb][t]])
            items.append(f)
        return items

    def mm_items(ST):
        hb = ST % 2
        hT = hTb[hb]
        items = []

        def ktype(blk, dst_fn, ta, tb):
            st_ = {}

            def ld():
                i = cnt["wk"] % 3
                cnt["wk"] += 1
                st_["i"] = i
                sc.dma("sp", wkt[i][:, :, :].rearrange("p k c -> p (k c)"), wink_b[blk], [Rwink[blk]], [Rwkt[i]], ds_ld)
            items.append(ld)
            for (a, b) in tile_groups(ta, tb):
                def grp(a=a, b=b):
                    i = st_["i"]
                    n = (b - a) * 128
                    pi = next_pb()
                    for k in range(16):
                        sc.op("pe", lambda h, k=k: h.matmul(pb[pi][:, 0:n], lhsT=wkt[i][:, k, :], rhs=hT[:, k, a * 128:b * 128], start=(k == 0), stop=(k == 15)),
                              [Rwkt[i]] + RhT[hb][a:b], [Rpb[pi]], defer=(k < 15))
                    dst, Rd = dst_fn(a, b)
                    evac_store(pb[pi][:, 0:n], lambda s_: s_[:, 0:n], dst, Rpb[pi], Rd, use_act=(cnt["stg"] % 2 == 0))
                items.append(grp)

        def vtype(wsrc, Rw, tiles, dst_fn, Rd):
            st_ = {}

            def ld():
                i = cnt["wv"] % 2
                cnt["wv"] += 1
                st_["i"] = i
                sc.dma("sp", wvt[i][:, :, :].rearrange("p k c -> p (k c)"), wsrc, [Rw], [Rwvt[i]], ds_ld)
            items.append(ld)
            for t in tiles:
                def tl(t=t):
                    i = st_["i"]
                    pi = next_pb()
                    for k in range(16):
                        sc.op("pe", lambda h, k=k: h.matmul(pb[pi][:, 0:512], lhsT=hT[:, k, t * 128:(t + 1) * 128], rhs=wvt[i][:, k, :], start=(k == 0), stop=(k == 15)),
                              [Rwvt[i], RhT[hb][t]], [Rpb[pi]], defer=(k < 15))
                    psap, shp, dst = dst_fn(ST * 8 + t, pb[pi])
                    evac_store(psap, shp, dst, Rpb[pi], Rd, use_act=(cnt["stg"] % 2 == 0))
                items.append(tl)

        for b8 in range(8):
            ktype(8 + b8, lambda a, b, b8=b8: (hk[b8, :, (ST * 8 + a) * 128:(ST * 8 + b) * 128], rs(("hk", b8))), 0, 8)

        def vdst(dram, nh, h0):
            return lambda T, p_: (p_[:, 0:512].rearrange("p (h c) -> p h c", h=nh), (lambda s_: s_[:, 0:512].rearrange("p (h c) -> p h c", h=nh)),
                                  dram(T)[h0:h0 + nh].rearrange("h r c -> r h c"))

        for u in range(2):
            vtype(winv_b[u], Rwinv[u], list(range(8)), vdst(lambda T: hv[:, T * 128:(T + 1) * 128, :], 2, 2 * u), rs(("hv", u)))
        dil_tiles = [t for t in range(8) if (ST * 8 + t) >= 119 or (ST * 8 + t) <= 24]
        q_tiles = [t for t in range(8) if (ST * 8 + t) == 127 or (ST * 8 + t) <= 16]
        if dil_tiles:
            ta, tb = dil_tiles[0], dil_tiles[-1] + 1
            for b8 in range(8):
                ktype(24 + b8, lambda a, b, b8=b8: (lk[b8, :, dkcol(ST * 8 + a):dkcol(ST * 8 + a) + (b - a) * 128], rs(("lk", b8))), ta, tb)
            for jj in range(2):
                vtype(winlv_b[jj], Rwinlv[jj], dil_tiles, vdst(lambda T: lv[:, dkcol(T):dkcol(T) + 128, :], 4, 4 * jj), rs(("lv", jj)))
        if q_tiles:
            ta, tb = q_tiles[0], q_tiles[-1] + 1
            for b8 in range(8):
                ktype(b8, lambda a, b, b8=b8: (qd[b8, :, ecol(ST * 8 + a):ecol(ST * 8 + a) + (b - a) * 128], rs(("qd", b8))), ta, tb)
                ktype(16 + b8, lambda a, b, b8=b8: (ql[b8, :, ecol(ST * 8 + a):ecol(ST * 8 + a) + (b - a) * 128], rs(("ql", b8))), ta, tb)
        return items

    NST = int(os.environ.get("NST", "16"))
    for f in norm_items(0):
        f()
    for ST in range(NST):
        mm = mm_items(ST)
        nx = norm_items(ST + 1) if ST + 1 < NST else []
        step = max(1, len(mm) // (len(nx) + 1)) if nx else len(mm) + 1
        ni = 0
        for idx, it in enumerate(mm):
            it()
            if nx and (idx + 1) % step == 0 and ni < len(nx):
                nx[ni]()
                ni += 1
        while ni < len(nx):
            nx[ni]()
            ni += 1

    sc.barrier()
    es.close()

    es = ExitStack()
    pb, Rpb, tp, Rtp = psum_alloc(es, 7, 1024)
    cast_w(wout_b.rearrange("p (o c) -> o p c", o=1), wout.rearrange("p (o c) -> o p c", o=1), 1, 16 * DM, Rwout)
    cast_w(wgu_b, wgu, 88, 16 * 128, Rwgu)
    cast_w(wdn_b, wdn, 8, NFC * 256, Rwdn)
    KT = [A("KT0", [128, S], BF16), A("KT1", [128, S], BF16)]
    RKT = [R(), R()]
    vt = A("vt", [128, 128, 257], BF16)
    Rvt = R()
    QT = [A("QT0", [128, EXT], BF16), A("QT1", [128, EXT], BF16)]
    RQT = [R(), R()]
    tz_sb = A("tz_sb", [128, 3, TZW], F32)
    cb_sb = A("cb_sb", [128, 768], F32)
    Rtz = R()
    pT = [A(f"pT{i}", [128, 384], BF16) for i in range(5)]
    RpT = [R() for _ in range(5)]
    sbt = [A(f"sbt{i}", [128, 384], F32) for i in range(3)]
    Rsbt = [R() for _ in range(3)]
    Amap = [A(f"Amap{i}", [128, 3, 256], F32) for i in range(2)]
    RA = [[R() for _ in range(3)] for _ in range(2)]
    ot = A("ot", [128, 256], F32)
    Rot = R()
    onb = A("onb", [128, 256], BF16)
    Ronb = R()
    ost = [A(f"ost{i}", [128, 2, 128], BF16) for i in range(2)]
    Rost = [R(), R()]
    junk2 = A("junk2", [128, 256], BF16)
    Rj2 = R()
    sm2 = A("sm2", [128, 16], F32)
    Rs2 = [R() for _ in range(8)]
    sc.op("pool", lambda h: h.memset(vt[:, :, 256:257], 1.0), [], [Rvt])
    c2 = {"s": 0, "p": 0, "sb": 0, "o": 0}
    for hh in range(4):
        for m in range(2):
            sc.dma("sp", KT[m][:, :], hk[2 * hh + m], [rs(("hk", 2 * hh + m))], [RKT[m]], ds_ld)
            sc.dma("sp", QT[m][:, :], qd[2 * hh + m], [rs(("qd", 2 * hh + m))], [RQT[m]], ds_ld)
        for g8 in range(8):
            sc.dma("sp", vt[:, g8 * 16:(g8 + 1) * 16, 0:256], hv[hh, g8 * 2048:(g8 + 1) * 2048, :].rearrange("(kb p) c -> p kb c", p=128),
                   [rs(("hv", hh // 2))], [Rvt], ds_ld)
        sc.dma("sp", tz_sb[:, :, :].rearrange("p a b -> p (a b)"), tzi[hh], [], [Rtz], ds_ld)
        sc.dma("sp", cb_sb[:, :], cbi[hh], [], [Rtz], ds_ld)
        for qb in range(6):
            for m in range(2):
                qs = QT[m][:, qb * 384:(qb + 1) * 384]

                def qk(kb):
                    si = 3 + (c2["s"] % 4)
                    c2["s"] += 1
                    sc.op("pe", lambda h: h.matmul(pb[si][:, 0:384], lhsT=KT[m][:, kb * 128:(kb + 1) * 128], rhs=qs, start=True, stop=True),
                          [RKT[m], RQT[m]], [Rpb[si]])
                    return si

                def expo(kb, si):
                    pi_ = c2["p"] % 5
                    c2["p"] += 1
                    var, st = _diff_tile_kind(kb, qb)
                    if var is None:
                        sc.op("act", lambda h: h.activation(out=pT[pi_][:, :], in_=pb[si][:, 0:384], func=AF.Exp,
                                                            bias=cb_sb[:, qb * 128 + kb:qb * 128 + kb + 1], scale=SCALE),
                              [Rpb[si], Rtz], [RpT[pi_]])
                    else:
                        bi = c2["sb"] % 3
                        c2["sb"] += 1
                        sc.op("dve", lambda h: h.scalar_tensor_tensor(out=sbt[bi][:, :], in0=pb[si][:, 0:384], scalar=SCALE,
                                                                      in1=tz_sb[:, var, st:st + 384], op0=ALU.mult, op1=ALU.add),
                              [Rpb[si], Rtz], [Rsbt[bi]])
                        sc.op("act", lambda h: h.activation(out=pT[pi_][:, :], in_=sbt[bi][:, :], func=AF.Exp), [Rsbt[bi]], [RpT[pi_]])
                    return pi_

                def pv(kb, pi_):
                    for j in range(3):
                        sc.op("pe", lambda h, j=j: h.matmul(pb[j][:, 0:257], lhsT=pT[pi_][:, j * 128:(j + 1) * 128], rhs=vt[:, kb, :],
                                                           start=(kb == 0), stop=(kb == 127)),
                              [RpT[pi_], Rvt], [Rpb[j]], defer=(j < 2))

                pend = []
                for kb in range(128):
                    pend.append((kb, qk(kb)))
                    if len(pend) > 3:
                        k0, s0 = pend.pop(0)
                        pv(k0, expo(k0, s0))
                while pend:
                    k0, s0 = pend.pop(0)
                    pv(k0, expo(k0, s0))
                for j in range(3):
                    sc.op("dve", lambda h, j=j: h.reciprocal(out=sm2[:, j:j + 1], in_=pb[j][:, 256:257]), [Rpb[j]], [Rs2[j]])
                    sc.op("dve", lambda h, j=j: h.tensor_scalar(out=Amap[m][:, j, :], in0=pb[j][:, 0:256], scalar1=sm2[:, j:j + 1], scalar2=None, op0=ALU.mult),
                          [Rpb[j], Rs2[j]], [RA[m][j]])
            for j in range(3):
                et = qb * 3 + j
                sc.op("dve", lambda h, j=j: h.scalar_tensor_tensor(out=ot[:, :], in0=Amap[1][:, j, :], scalar=neglam[:, 0:1], in1=Amap[0][:, j, :],
                                                                  op0=ALU.mult, op1=ALU.add), [RA[0][j], RA[1][j], Rc], [Rot])
                sc.op("act", lambda h: h.activation(out=junk2[:, :], in_=ot[:, :], func=AF.Square, accum_out=sm2[:, 4:5]), [Rot], [Rj2, Rs2[4]])
                sc.op("dve", lambda h: h.tensor_scalar(out=sm2[:, 5:6], in0=sm2[:, 4:5], scalar1=1.0 / 256, scalar2=1e-5, op0=ALU.mult, op1=ALU.add), [Rs2[4]], [Rs2[5]])
                sc.op("act", lambda h: h.activation(out=sm2[:, 6:7], in_=sm2[:, 5:6], func=AF.Sqrt), [Rs2[5]], [Rs2[6]])
                sc.op("dve", lambda h: h.reciprocal(out=sm2[:, 7:8], in_=sm2[:, 6:7]), [Rs2[6]], [Rs2[7]])
                sc.op("dve", lambda h: h.scalar_tensor_tensor(out=onb[:, :], in0=ot[:, :], scalar=sm2[:, 7:8], in1=subg8[:, :], op0=ALU.mult, op1=ALU.mult),
                      [Rot, Rs2[7], Rc], [Ronb])
                for c_ in range(2):
                    sc.op("pe", lambda h, c_=c_: h.transpose(out=tp[:, c_ * 128:(c_ + 1) * 128], in_=onb[:, c_ * 128:(c_ + 1) * 128], identity=ident[:, :]),
                          [Ronb, Rc], [Rtp], defer=(c_ < 1))
                oi = c2["o"] % 2
                c2["o"] += 1
                sc.op("act", lambda h: h.copy(out=ost[oi][:, :, :], in_=tp[:, 0:256].rearrange("p (c t) -> p c t", c=2)), [Rtp], [Rost[oi]])
                sc.dma("pool", odT[2 * hh:2 * hh + 2, :, et * 128:(et + 1) * 128].rearrange("c p t -> p c t"), ost[oi][:, :, :], [Rost[oi]], [rs(("od", et))], ds_st)
    sc.barrier()
    es.close()

    es = ExitStack()
    pb, Rpb, tp, Rtp = psum_alloc(es, 8, 0)
    KlTb = [A(f"KlT{i}", [128, DKN], BF16) for i in range(2)]
    QlTb = [A(f"QlT{i}", [128, EXT], BF16) for i in range(2)]
    RKlb, RQlb = [R(), R()], [R(), R()]
    bm_sbb = [A(f"bm_sb{i}", [128, 3, 2, 128], F32) for i in range(2)]
    kv_sb = A("kv_sb", [128, NCH], F32)
    Rbmb, Rkv = [R(), R()], R()
    vchb = [A(f"vch{i}", [128, NCH, 128], BF16) for i in range(2)]
    Rvchb = [R(), R()]
    accUD = A("accUD", [128, 2, EXT], F32)
    accU = accUD[:, 0, :]
    accD = accUD[:, 1, :]
    RaU = R()
    RaD = RaU
    Rss_ = [R() for _ in range(8)]
    Rso_ = [R() for _ in range(8)]
    sq = A("sq", [128, EXT], F32)
    Rsq = R()
    rst = A("rst", [128, EXT], F32)
    Rrst = R()
    sb3 = [A(f"sb3{i}", [128, 2, 128], F32) for i in range(6)]
    Rsb3 = [R() for _ in range(6)]
    pT3 = [A(f"pT3{i}", [128, 2, 128], BF16) for i in range(6)]
    RpT3 = [R() for _ in range(6)]
    olb = [A(f"olb{i}", [128, 512], BF16) for i in range(2)]
    Rolb = [R(), R()]
    sc.dma("sp", kv_sb[:, :], kvi[:, :], [], [Rkv], ds_ld)
    c3 = {"b": 0, "o": 0}

    def dil_loads(hl):
        hb = hl % 2
        sc.dma("sp", KlTb[hb][:, :], lk[hl], [rs(("lk", hl))], [RKlb[hb]], ds_ld)
        sc.dma("sp", QlTb[hb][:, :], ql[hl], [rs(("ql", hl))], [RQlb[hb]], ds_ld)
        sc.dma("sp", bm_sbb[hb][:, :, :, :].rearrange("p a b c -> p (a b c)"), bmi[hl], [], [Rbmb[hb]], ds_ld)
        for p_, (_, d) in enumerate(PATS):
            nq = EXT // d
            nblk = -(-nq // 128)
            nch = nblk + 1
            for c in range(d):
                g0 = base[(p_, c)]
                row0 = c - 64 * d + 1024
                src = lv[hl, row0:row0 + d * 128 * nch:d, :].rearrange("(i j) c -> j i c", j=128)
                sc.dma("sp", vchb[hb][:, g0:g0 + nch, :], src, [rs(("lv", hl // 4))], [Rvchb[hb]], ds_ld)

    dil_loads(0)
    for hl in range(8):
        hb = hl % 2
        KlT, QlT, RKl, RQl = KlTb[hb], QlTb[hb], RKlb[hb], RQlb[hb]
        bm_sb, Rbm, vch, Rvch = bm_sbb[hb], Rbmb[hb], vchb[hb], Rvchb[hb]
        if hl + 1 < 8:
            dil_loads(hl + 1)
        sc.op("pool", lambda h: h.memset(accUD[:, :, :], 0.0), [], [RaU])
        blocks = []
        for p_, (_, d) in enumerate(PATS):
            nq = EXT // d
            nblk = -(-nq // 128)
            for c in range(d):
                for blk in range(nblk):
                    blocks.append((p_, d, c, blk, nq))

        def stage1(B):
            p_, d, c, blk, nq = B
            u0 = blk * 128
            n = min(128, nq - u0)
            kA = c + d * (u0 - 64) + 1024
            kB = kA + 128 * d
            q0 = c + d * u0
            qs = QlT[:, q0:q0 + d * (n - 1) + 1:d]
            j = c3["b"] % 4
            bi = c3["b"] % 6
            c3["b"] += 1
            ps_s = pb[j][:, 0:256]
            sc.op("pe", lambda h: h.matmul(ps_s[:, 0:n], lhsT=KlT[:, kA:kA + d * 127 + 1:d], rhs=qs, start=True, stop=True),
                  [RKl, RQl], [Rss_[j]], defer=True)
            sc.op("pe", lambda h: h.matmul(ps_s[0:n, 128:128 + n], lhsT=KlT[:, kB:kB + d * (n - 1) + 1:d], rhs=qs, start=True, stop=True),
                  [RKl, RQl], [Rss_[j]])
            return (bi, n, q0, j)

        def stage2(B, st):
            p_, d, c, blk, nq = B
            bi, n, q0, j = st
            ps_s = pb[j][:, 0:256]
            gi = base[(p_, c)] + blk
            if n == 128:
                sc.op("dve", lambda h: h.scalar_tensor_tensor(out=sb3[bi][:, :, :].rearrange("p a b -> p (a b)"), in0=ps_s[:, 0:256], scalar=SCALE,
                                                              in1=bm_sb[:, p_, :, :].rearrange("p a b -> p (a b)"), op0=ALU.mult, op1=ALU.add),
                      [Rss_[j], Rbm], [Rsb3[bi]])
            else:
                sc.op("dve", lambda h: h.scalar_tensor_tensor(out=sb3[bi][:, 0, 0:n], in0=ps_s[:, 0:n], scalar=SCALE, in1=bm_sb[:, p_, 0, 0:n],
                                                              op0=ALU.mult, op1=ALU.add), [Rss_[j], Rbm], [Rsb3[bi]])
                sc.op("dve", lambda h: h.scalar_tensor_tensor(out=sb3[bi][0:n, 1, 0:n], in0=ps_s[0:n, 128:128 + n], scalar=SCALE, in1=bm_sb[0:n, p_, 1, 0:n],
                                                              op0=ALU.mult, op1=ALU.add), [Rss_[j], Rbm], [Rsb3[bi]])
            sc.op("act", lambda h: h.activation(out=pT3[bi][:, 0, 0:n], in_=sb3[bi][:, 0, 0:n], func=AF.Exp, bias=kv_sb[:, gi:gi + 1]),
                  [Rsb3[bi], Rkv], [RpT3[bi]])
            sc.op("act", lambda h: h.activation(out=pT3[bi][0:n, 1, 0:n], in_=sb3[bi][0:n, 1, 0:n], func=AF.Exp, bias=kv_sb[0:n, gi + 1:gi + 2]),
                  [Rsb3[bi], Rkv], [RpT3[bi]])

        def stage3(B, st):
            p_, d, c, blk, nq = B
            bi, n, q0, j = st
            ps_o = pb[4 + j][:, 0:256]
            gi = base[(p_, c)] + blk
            sc.op("pe", lambda h: h.matmul(ps_o[:, 0:n], lhsT=vch[:, gi, :], rhs=pT3[bi][:, 0, 0:n], start=True, stop=False),
                  [Rvch, RpT3[bi]], [Rso_[j]], defer=True)
            sc.op("pe", lambda h: h.matmul(ps_o[:, 0:n], lhsT=vch[0:n, gi + 1, :], rhs=pT3[bi][0:n, 1, 0:n], start=False, stop=True),
                  [Rvch, RpT3[bi]], [Rso_[j]], defer=True)
            sc.op("pe", lambda h: h.matmul(ps_o[:, 128:128 + n], lhsT=ones_bf[:, :], rhs=pT3[bi][:, 0, 0:n], start=True, stop=False),
                  [Rc, RpT3[bi]], [Rso_[j]], defer=True)
            sc.op("pe", lambda h: h.matmul(ps_o[:, 128:128 + n], lhsT=ones_bf[0:n, :], rhs=pT3[bi][0:n, 1, 0:n], start=False, stop=True),
                  [Rc, RpT3[bi]], [Rso_[j]])
            esl = slice(q0, q0 + d * (n - 1) + 1, d)
            sc.op("dve", lambda h: h.tensor_tensor(out=accUD[:, :, esl], in0=accUD[:, :, esl], in1=ps_o.rearrange("p (a b) -> p a b", a=2)[:, :, 0:n], op=ALU.add),
                  [Rso_[j], RaU], [RaU])

        nb_ = len(blocks)
        sts = {}
        for i in range(nb_ + 3):
            if i < nb_:
                sts[i] = stage1(blocks[i])
            if 0 <= i - 2 < nb_:
                stage2(blocks[i - 2], sts[i - 2])
            if 0 <= i - 3 < nb_:
                stage3(blocks[i - 3], sts[i - 3])
        sc.op("dve", lambda h: h.tensor_scalar(out=accUD[:, 1, :], in0=accUD[:, 1, :], scalar1=1e-18, scalar2=None, op0=ALU.max), [RaD], [RaD])
        sc.op("act", lambda h: h.activation(out=accUD[:, 1, :], in_=accUD[:, 1, :], func=AF.Ln), [RaD], [RaD])
        sc.op("act", lambda h: h.activation(out=accUD[:, 1, :], in_=accUD[:, 1, :], func=AF.Exp, scale=-1.0), [RaD], [RaD])
        sc.op("dve", lambda h: h.tensor_tensor(out=accUD[:, 0, :], in0=accUD[:, 0, :], in1=accUD[:, 1, :], op=ALU.mult), [RaU, RaD], [RaU])
        sc.op("pool", lambda h: h.tensor_tensor(out=sq[:, :], in0=accUD[:, 0, :], in1=accUD[:, 0, :], op=ALU.mult), [RaU], [Rsq])
        for c0 in range(0, EXT, 512):
            n = min(512, EXT - c0)
            pi = c3["o"] % 4
            sc.op("pe", lambda h: h.matmul(pb[pi][:, 0:n], lhsT=ones_f[:, :], rhs=sq[:, c0:c0 + n], start=True, stop=True), [Rc, Rsq], [Rss_[pi]])
            sc.op("dve", lambda h: h.tensor_scalar(out=rst[:, c0:c0 + n], in0=pb[pi][:, 0:n], scalar1=1.0 / 128, scalar2=1e-6, op0=ALU.mult, op1=ALU.add),
                  [Rss_[pi]], [Rrst])
            sc.op("act", lambda h: h.activation(out=rst[:, c0:c0 + n], in_=rst[:, c0:c0 + n], func=AF.Ln), [Rrst], [Rrst])
            sc.op("act", lambda h: h.activation(out=rst[:, c0:c0 + n], in_=rst[:, c0:c0 + n], func=AF.Exp, scale=-0.5), [Rrst], [Rrst])
            oi = c3["o"] % 2
            c3["o"] += 1
            sc.op("dve", lambda h: h.scalar_tensor_tensor(out=olb[oi][:, 0:n], in0=accUD[:, 0, c0:c0 + n], scalar=misc_sb[:, O_DILG + hl:O_DILG + hl + 1],
                                                          in1=rst[:, c0:c0 + n], op0=ALU.mult, op1=ALU.mult), [RaU, Rrst, Rc], [Rolb[oi]])
            sc.dma("pool", odT[8 + hl, :, c0:c0 + n], olb[oi][:, 0:n], [Rolb[oi]], [rs(("ol", hl))], ds_st)
    sc.barrier()
    es.close()

    es = ExitStack()
    pb, Rpb, tp, Rtp = psum_alloc(es, 6, 2048)
    Wo = A("Wo", [128, 16, DM], BF16)
    RWo = R()
    g2 = A("g2", [128, DM], F32)
    gf = A("gf", [128, DM], F32)
    Rg2 = R()
    sc.dma("sp", Wo[:, :, :].rearrange("p k c -> p (k c)"), wout_b[:, :], [Rwout[0]], [RWo], ds_ld)
    sc.dma("sp", g2[:, :], gains[1], [], [Rg2], ds_c)
    sc.dma("sp", gf[:, :], gains[2], [], [Rg2], ds_c)
    Rh2 = R()
    sc.dma("pool", h2s[:, :, 0:1].rearrange("k p o -> p k o"), zeros_bf[:, 0:16].rearrange("p (k o) -> p k o", o=1), [Rc], [Rh2], ds_st, slow=True)
    sc.dma("pool", h2s[:, :, EXT + 1:EXT + 2].rearrange("k p o -> p k o"), zeros_bf[:, 0:16].rearrange("p (k o) -> p k o", o=1), [Rc], [Rh2], ds_st, slow=True)
    OTb = [A(f"OT{i}", [128, 16, 128], BF16) for i in range(2)]
    ROT = [R(), R()]
    xt4 = [A(f"xt4{i}", [128, DM], F32) for i in range(2)]
    Rxt4 = [R(), R()]
    x1b = [A(f"x1b{i}", [128, DM], F32) for i in range(2)]
    Rx1 = [R(), R()]
    h2b = [A(f"h2b{i}", [128, DM], BF16) for i in range(2)]
    Rh2b = [R(), R()]
    hst = [A(f"hst{i}", [128, 16, 128], BF16) for i in range(2)]
    Rhst = [R(), R()]
    junk4 = A("junk4", [128, DM], BF16)
    ssb4 = A("ssb4", [128, 4], F32)
    junk, Rjunk, ssb = junk4, R(), ssb4
    Rss[0], Rss[1], Rrs[0], Rrs[1] = R(), R(), R(), R()
    Rx1s = [R() for _ in range(NET)]
    for et in range(NET):
        i2 = et % 2
        r0 = (S - 128) if et == 0 else (et - 1) * 128
        od_deps = [rs(("od", et))] + [rs(("ol", h_)) for h_ in range(8)]
        sc.dma("sp", OTb[i2][:, :, :], odT[:, :, et * 128:(et + 1) * 128].rearrange("c p t -> p c t"), od_deps, [ROT[i2]], ds_ld)
        sc.dma("sp", xt4[i2][:, :], xr[r0:r0 + 128, :], [], [Rxt4[i2]], ds_x)
        for db in range(4):
            for ci in range(16):
                sc.op("pe", lambda h, ci=ci: h.matmul(pb[db][:, :], lhsT=OTb[i2][:, ci, :], rhs=Wo[:, ci, db * 512:(db + 1) * 512], start=(ci == 0), stop=(ci == 15)),
                      [ROT[i2], RWo], [Rpb[db]], defer=(ci < 15))
            sc.op("dve", lambda h, db=db: h.tensor_tensor(out=x1b[i2][:, db * 512:(db + 1) * 512], in0=pb[db][:, :], in1=xt4[i2][:, db * 512:(db + 1) * 512], op=ALU.add),
                  [Rpb[db], Rxt4[i2]], [Rx1[i2]])
        sc.dma("pool", x1s[et * 128:(et + 1) * 128, :], x1b[i2][:, :], [Rx1[i2]], [Rx1s[et]], ds_st)
        norm_tile(None, x1b[i2], Rx1[i2], g2, Rg2, h2b[i2], Rh2b[i2], i2, extra_scale=misc_sb[:, O_TM + et:O_TM + et + 1])
        transpose16(h2b[i2], Rh2b[i2], hst[i2][:, :, :], [Rhst[i2]])
        sc.dma("pool", h2s[:, :, 1 + et * 128:1 + (et + 1) * 128].rearrange("k p t -> p k t"), hst[i2][:, :, :], [Rhst[i2]], [Rh2], ds_st)
    sc.barrier()
    es.close()

    es = ExitStack()
    pb, Rpb, tp, Rtp = psum_alloc(es, 8, 0)
    gf = A("gfb", [128, DM], F32)
    Rgf = R()
    sc.dma("sp", gf[:, :], gains[2], [], [Rgf], ds_c)
    h2g = [A(f"h2g{i}", [128, 16, 514], BF16) for i in range(1)] * 2
    Rh2g = [R()] * 2
    x1g = [A(f"x1g{i}", [128, 4, DM], F32) for i in range(1)] * 2
    Rx1g = [[R() for _ in range(4)]] * 2
    wgt = [A(f"wgt{i}", [128, 16, 128], BF16) for i in range(3)]
    wut = [A(f"wut{i}", [128, 16, 128], BF16) for i in range(3)]
    Rwgt = [R() for _ in range(3)]
    Rwut = [R() for _ in range(3)]
    aT = A("aT", [128, NFC, 512], BF16)
    RaT = [R() for _ in range(NFC)]
    wds = [A(f"wds{i}", [128, NFC, 256], BF16) for i in range(2)]
    Rwds = [R(), R()]
    ca = [A(f"ca{i}", [128, 512], F32) for i in range(4)]
    Rca = [R() for _ in range(4)]
    yo = [A(f"yo{i}", [128, DM], F32) for i in range(1)] * 2
    Ryo = [R()] * 2
    junk5 = A("junk5", [128, DM], BF16)
    Rj5 = R()
    sm5 = A("sm5", [128, 8], F32)
    Rs5 = [R() for _ in range(4)]
    Ry = R()
    Rhalo = [R(), R()]
    c4 = {"w": 0, "d": 0, "y": 0}
    CW, CB = O_CW, O_CB
    for g in range(4):
        gi2 = g % 2
        e0 = 128 + 512 * g
        sc.dma("sp", h2g[gi2][:, :, :], h2s[:, :, e0:e0 + 514].rearrange("k p t -> p k t"), [Rh2], [Rh2g[gi2]], ds_ld)
        for tt in range(4):
            et = 1 + g * 4 + tt
            sc.dma("sp", x1g[gi2][:, tt, :], x1s[et * 128:(et + 1) * 128, :], [Rx1s[et]], [Rx1g[gi2][tt]], ds_x)
        for fc in range(NFC):
            wi = c4["w"] % 3
            c4["w"] += 1
            sc.dma("sp", wgt[wi][:, :, :].rearrange("p k c -> p (k c)"), wgu_b[fc], [Rwgu[fc]], [Rwgt[wi]], ds_ld)
            sc.dma("sp", wut[wi][:, :, :].rearrange("p k c -> p (k c)"), wgu_b[NFC + fc], [Rwgu[NFC + fc]], [Rwut[wi]], ds_ld)
            pg = fc % 2
            pu = 2 + fc % 2
            hi = fc % 2
            ph = pb[6 + hi][:, 0:2]
            for k in range(16):
                sc.op("pe", lambda h, k=k: h.matmul(pb[pg][:, :], lhsT=wgt[wi][:, k, :], rhs=h2g[gi2][:, k, 1:513], start=(k == 0), stop=(k == 15)),
                      [Rwgt[wi], Rh2g[gi2]], [Rpb[pg]], defer=True)
                sc.op("pe", lambda h, k=k: h.matmul(ph, lhsT=wgt[wi][:, k, :], rhs=h2g[gi2][:, k, 0:514:513], start=(k == 0), stop=(k == 15)),
                      [Rwgt[wi], Rh2g[gi2]], [Rhalo[hi]], defer=(k < 15))
            for k in range(16):
                sc.op("pe", lambda h, k=k: h.matmul(pb[pu][:, :], lhsT=wut[wi][:, k, :], rhs=h2g[gi2][:, k, 1:513], start=(k == 0), stop=(k == 15)),
                      [Rwut[wi], Rh2g[gi2]], [Rpb[pu]], defer=(k < 15))
            cw0 = misc_sb[:, CW + fc * 3 + 0:CW + fc * 3 + 1]
            cw1 = misc_sb[:, CW + fc * 3 + 1:CW + fc * 3 + 2]
            cw2 = misc_sb[:, CW + fc * 3 + 2:CW + fc * 3 + 3]
            cbb = misc_sb[:, CB + fc:CB + fc + 1]
            sc.op("act", lambda h: h.activation(out=ca[0][:, :], in_=pb[pg][:, :], func=AF.Identity, bias=cbb, scale=cw1), [Rpb[pg], Rc], [Rca[0]])
            sc.op("dve", lambda h: h.scalar_tensor_tensor(out=ca[1][:, 1:512], in0=pb[pg][:, 0:511], scalar=cw0, in1=ca[0][:, 1:512], op0=ALU.mult, op1=ALU.add),
                  [Rpb[pg], Rca[0], Rc], [Rca[1]])
            sc.op("dve", lambda h: h.scalar_tensor_tensor(out=ca[1][:, 0:1], in0=ph[:, 0:1], scalar=cw0, in1=ca[0][:, 0:1], op0=ALU.mult, op1=ALU.add),
                  [Rhalo[hi], Rca[0], Rc], [Rca[1]])
            sc.op("dve", lambda h: h.scalar_tensor_tensor(out=ca[2][:, 0:511], in0=pb[pg][:, 1:512], scalar=cw2, in1=ca[1][:, 0:511], op0=ALU.mult, op1=ALU.add),
                  [Rpb[pg], Rca[1], Rc], [Rca[2]])
            sc.op("dve", lambda h: h.scalar_tensor_tensor(out=ca[2][:, 511:512], in0=ph[:, 1:2], scalar=cw2, in1=ca[1][:, 511:512], op0=ALU.mult, op1=ALU.add),
                  [Rhalo[hi], Rca[1], Rc], [Rca[2]])
            sc.op("act", lambda h: h.activation(out=ca[3][:, :], in_=ca[2][:, :], func=AF.Silu), [Rca[2]], [Rca[3]])
            sc.op("dve", lambda h: h.tensor_tensor(out=aT[:, fc, :], in0=ca[3][:, :], in1=pb[pu][:, :], op=ALU.mult), [Rca[3], Rpb[pu]], [RaT[fc]])
        for db8 in range(8):
            di = c4["d"] % 2
            c4["d"] += 1
            sc.dma("sp", wds[di][:, :, :].rearrange("p f c -> p (f c)"), wdn_b[db8], [Rwdn[db8]], [Rwds[di]], ds_ld)
            for tt in range(4):
                pi = 4 + (tt + db8) % 2
                for fc in range(NFC):
                    sc.op("pe", lambda h, fc=fc: h.matmul(pb[pi][:, 0:256], lhsT=aT[:, fc, tt * 128:(tt + 1) * 128], rhs=wds[di][:, fc, :], start=(fc == 0), stop=(fc == NFC - 1)),
                          [RaT[fc], Rwds[di]], [Rpb[pi]], defer=(fc < NFC - 1))
                xs_ = x1g[gi2][:, tt, db8 * 256:(db8 + 1) * 256]
                sc.op("dve", lambda h: h.tensor_tensor(out=xs_, in0=pb[pi][:, 0:256], in1=xs_, op=ALU.add), [Rpb[pi], Rx1g[gi2][tt]], [Rx1g[gi2][tt]])
        for tt in range(4):
            et = 1 + g * 4 + tt
            yi = c4["y"] % 2
            c4["y"] += 1
            xs_ = x1g[gi2][:, tt, :]
            sc.op("act", lambda h: h.activation(out=junk5[:, :], in_=xs_, func=AF.Square, accum_out=sm5[:, 0:1]), [Rx1g[gi2][tt]], [Rj5, Rs5[0]])
            sc.op("dve", lambda h: h.tensor_scalar(out=sm5[:, 1:2], in0=sm5[:, 0:1], scalar1=1.0 / DM, scalar2=1e-6, op0=ALU.mult, op1=ALU.add), [Rs5[0]], [Rs5[1]])
            sc.op("act", lambda h: h.activation(out=sm5[:, 2:3], in_=sm5[:, 1:2], func=AF.Sqrt), [Rs5[1]], [Rs5[2]])
            sc.op("dve", lambda h: h.reciprocal(out=sm5[:, 3:4], in_=sm5[:, 2:3]), [Rs5[2]], [Rs5[3]])
            sc.op("dve", lambda h: h.scalar_tensor_tensor(out=yo[yi][:, :], in0=xs_, scalar=sm5[:, 3:4], in1=gf[:, :], op0=ALU.mult, op1=ALU.mult),
                  [Rx1g[gi2][tt], Rs5[3], Rgf], [Ryo[yi]])
            sc.dma("pool", y[(et - 1) * 128:et * 128, :], yo[yi][:, :], [Ryo[yi]], [Ry], ds_st)
    sc.barrier()
    es.close()
    return nc


def kernel(x, norm1_gain, w_in, rel_bias_table, lambda_q1, lambda_k1, lambda_q2, lambda_k2,
           diff_subln_gain, dil_out_gain, w_out, norm2_gain, w_gate_up, conv_w, conv_b,
           w_down, final_gain):
    f = np.float32
    x2 = np.asarray(x, f)[0]
    table = np.asarray(rel_bias_table, f)
    w_in0 = np.asarray(w_in, f)[0]
    blocks = np.ascontiguousarray(w_in0.reshape(16, 128, 48, 128).transpose(2, 1, 0, 3)).reshape(48, 128, 2048)
    wink = np.ascontiguousarray(np.concatenate([blocks[0:8], blocks[8:16], blocks[24:32], blocks[32:40]], 0))

    def vblk(cols):
        return np.ascontiguousarray(cols.reshape(16, 128, 2, 512).transpose(2, 1, 0, 3)).reshape(2, 128, 16 * 512)

    winv = vblk(w_in0[:, 2048:3072])
    winlv = vblk(w_in0[:, 5120:6144])
    wout_h = np.ascontiguousarray(np.asarray(w_out, f)[0].reshape(16, 128, DM).transpose(1, 0, 2)).reshape(128, 16 * DM)
    wgu_h = np.ascontiguousarray(np.asarray(w_gate_up, f)[0].reshape(16, 128, 88, 128).transpose(2, 1, 0, 3)).reshape(88, 128, 2048)
    wdn_h = np.ascontiguousarray(np.asarray(w_down, f)[0].reshape(NFC, 128, 8, 256).transpose(2, 1, 0, 3)).reshape(8, 128, NFC * 256)
    gains = np.ascontiguousarray(np.stack([
        np.broadcast_to(np.asarray(norm1_gain, f)[0], (128, DM)),
        np.broadcast_to(np.asarray(norm2_gain, f)[0], (128, DM)),
        np.broadcast_to(np.asarray(final_gain, f), (128, DM))], 0))
    MISC = 512 + 256 + 8 + NFC * 3 + NFC + NET
    ident = np.eye(128, dtype=f)
    in_maps = []
    for c in range(NCORE):
        tz, cbv, bm, kv = _host_tables(c, table)
        misc = np.zeros((128, MISC), f)
        misc[:, 0:128] = np.asarray(lambda_q1, f)[0][None]
        misc[:, 128:256] = np.asarray(lambda_k1, f)[0][None]
        misc[:, 256:384] = np.asarray(lambda_q2, f)[0][None]
        misc[:, 384:512] = np.asarray(lambda_k2, f)[0][None]
        misc[:, 512:768] = np.asarray(diff_subln_gain, f)[0][None]
        misc[:, 768:776] = np.asarray(dil_out_gain, f)[0].reshape(8, 128).T
        misc[:, 776:776 + NFC * 3] = np.asarray(conv_w, f)[0].T.reshape(NFC, 128, 3).transpose(1, 0, 2).reshape(128, NFC * 3)
        misc[:, 776 + NFC * 3:776 + NFC * 4] = np.asarray(conv_b, f)[0].reshape(NFC, 128).T
        tm = np.ones(NET, f)
        if c == 0:
            tm[0] = 0.0
        if c == NCORE - 1:
            tm[NET - 1] = 0.0
        misc[:, 776 + NFC * 4:] = tm[None]
        in_maps.append({
            "xr": np.ascontiguousarray(np.roll(x2, -OWN * c, axis=0)),
            "gains": gains, "wink": wink, "winv": winv, "winlv": winlv, "wout": wout_h,
            "wgu": wgu_h, "wdn": wdn_h,
            "tz": np.ascontiguousarray(tz.reshape(4, 128, 3 * TZW)), "cbv": cbv,
            "bm": np.ascontiguousarray(bm.reshape(8, 128, 768)), "kv": kv,
            "misc": misc, "ident": ident,
        })
    nc = build_nc()
    res = run_bass_kernel_spmd(nc, in_maps, core_ids=list(range(NCORE)))
    out = np.concatenate([np.asarray(res.results[c]["y"], f) for c in range(NCORE)], 0)
    return out[None].astype(np.float32)
```

```python
import math
import os
from contextlib import ExitStack
import numpy as np
import ml_dtypes
import concourse.bass as bass
import concourse.mybir as mybir
from concourse.bass_utils import run_bass_kernel_spmd

F32 = mybir.dt.float32
BF16 = mybir.dt.bfloat16
AF = mybir.ActivationFunctionType
ALU = mybir.AluOpType
AX = mybir.AxisListType

S = 16384
DM = 2048
NCORE = 8
OWN = 2048
EXT = 2304
NET = 18
SCALE = 1.0 / math.sqrt(128.0)
J0 = 942
TZW = 2048
DKN = 4352
NFC = 44
NEG = -1e30
PATS = ((128, 1), (512, 4), (2048, 16))


class Res:
    __slots__ = ("w", "rs", "pend")

    def __init__(self):
        self.w = None
        self.rs = {}
        self.pend = None


class _Eng:
    pass


class _DSem:
    pass


class Sch:
    LIMIT = 30000

    def __init__(self, nc):
        self.nc = nc
        self.E = {}
        self.nsem = 0
        self.dsems = []
        self.owner_sems = {}
        self.owner_keep = []
        self.free_ds = {}
        for n, h in (("pe", nc.tensor), ("act", nc.scalar), ("dve", nc.vector),
                     ("pool", nc.gpsimd), ("sp", nc.sync)):
            e = _Eng()
            e.h = h
            e.name = n
            e.sem = self._newsem("e_" + n)
            e.own = {id(e.sem)}
            e.cnt = 0
            e.waited = {}
            e.deferred = []
            self.E[n] = e

    def _newsem(self, name):
        self.nsem += 1
        return self.nc.alloc_semaphore(f"{name}_{self.nsem}")

    def dsem(self):
        d = _DSem()
        d.h = self._newsem("d")
        d.cnt = 0
        self.dsems.append(d)
        return d

    def _wait(self, e, tok):
        if tok is None:
            return
        sem, val = tok
        k = id(sem)
        if e.name == "pe" and k in e.own:
            return
        if e.waited.get(k, (None, 0))[1] >= val:
            return
        e.h.wait_ge(sem, val)
        e.waited[k] = (sem, val)

    def _gather(self, e, reads, writes):
        for r in reads:
            assert r.pend is None or r.pend is e, "read of pending resource"
            self._wait(e, r.w)
        for w in writes:
            assert w.pend is None or w.pend is e, "write of pending resource"
            self._wait(e, w.w)
            for t in w.rs.values():
                self._wait(e, t)

    @staticmethod
    def _commit(tok, reads, writes):
        for r in reads:
            r.rs[id(tok[0])] = tok
            r.pend = None
        for w in writes:
            w.w = tok
            w.rs = {}
            w.pend = None

    def op(self, eng, fn, reads=(), writes=(), defer=False):
        e = self.E[eng]
        self._gather(e, reads, writes)
        ins = fn(e.h)
        if defer:
            e.deferred.append((tuple(reads), tuple(writes)))
            for r in reads:
                r.pend = e
            for w in writes:
                w.pend = e
            return
        if e.cnt >= self.LIMIT:
            e.sem = self._newsem("e_" + e.name)
            e.own.add(id(e.sem))
            e.cnt = 0
        e.cnt += 1
        ins.then_inc(e.sem, 1)
        tok = (e.sem, e.cnt)
        for rr, ww in e.deferred:
            self._commit(tok, rr, ww)
        e.deferred = []
        self._commit(tok, reads, writes)

    def dma(self, q, out, in_, reads, writes, sem, slow=False):
        e = self.E[q]
        owner = None
        if q == "sp" and writes:
            owner = writes[0]
        elif q == "pool" and getattr(sem, "is_store", False) and reads:
            owner = reads[0]
        if owner is not None:
            d = self.owner_sems.get((id(owner), q))
            if d is None:
                fl = self.free_ds.setdefault(q, [])
                d = fl.pop() if fl else self.dsem()
                self.owner_sems[(id(owner), q)] = d
                self.owner_keep.append(owner)
            sem = d
        self._gather(e, reads, writes)
        if sem.cnt + 16 > self.LIMIT:
            sem.h = self._newsem("d")
            sem.cnt = 0
        ins = e.h.dma_start(out=out, in_=in_, allow_slow_non_contiguous=True) if slow else e.h.dma_start(out=out, in_=in_)
        sem.cnt += 16
        ins.then_inc(sem.h, 16)
        self._commit((sem.h, sem.cnt), reads, writes)

    def prewait(self, eng, reads, writes):
        self._gather(self.E[eng], reads, writes)

    def barrier(self):
        toks = []
        for e in self.E.values():
            assert not e.deferred
            if e.cnt > 0:
                toks.append((e.sem, e.cnt))
        for d in self.dsems:
            if d.cnt > 0:
                toks.append((d.h, d.cnt))
        for e in self.E.values():
            for t in toks:
                self._wait(e, t)
        for (_, q_), d_ in self.owner_sems.items():
            self.free_ds.setdefault(q_, []).append(d_)
        self.owner_sems = {}


def _bucket(rel):
    rel = np.asarray(rel, dtype=np.int64)
    ret = np.where(rel > 0, 16, 0)
    n = np.abs(rel)
    nf = np.maximum(n, 1).astype(np.float32)
    large = 8 + (np.log(nf / np.float32(8)) / np.float32(math.log(128.0)) * np.float32(8)).astype(np.int32)
    large = np.minimum(large, 15)
    return ret + np.where(n < 8, n, large)


def _diff_tile_kind(kb, qb):
    D = 128 * kb - (384 * qb - 128)
    if kb >= 120 and (D - S) > -686:
        return 1, J0 - (D - S)
    if -686 < D < 942:
        return (0 if kb < 16 else 2), J0 - D
    return None, None


def _dil_chunks():
    base = {}
    gi = 0
    for p, (_, d) in enumerate(PATS):
        nq = EXT // d
        nblk = -(-nq // 128)
        for c in range(d):
            base[(p, c)] = gi
            gi += nblk + 1
    return base, gi


def _host_tables(core, table):
    tdiff = table[:, :4]
    tdil = table[:, 4:]
    p = np.arange(128)[:, None]
    j = np.arange(TZW)[None, :]
    rho = p - j + J0
    bk = _bucket(rho)
    tz = np.zeros((4, 128, 3, TZW), np.float32)
    for h in range(4):
        f = tdiff[bk, h]
        tz[h, :, 0] = f
        tz[h, :, 1] = f if core >= 1 else tdiff[31, h]
        tz[h, :, 2] = f if core < 7 else tdiff[15, h]
    cbv = np.zeros((4, 128, 6 * 128), np.float32)
    for qb in range(6):
        tq = (384 * qb - 128 + 192) + 2048 * core
        for kb in range(128):
            tk = (128 * kb + 64 + 2048 * core) % S
            b = 31 if tk > tq else 15
            cbv[:, :, qb * 128 + kb] = tdiff[b, :][:, None]
    bm = np.full((8, 128, 3, 2, 128), NEG, np.float32)
    lane = np.arange(128)[:, None]
    q = np.arange(128)[None, :]
    for pi, (_, d) in enumerate(PATS):
        for ch in range(2):
            off = lane - 64 - q if ch == 0 else lane + 64 - q
            ok = np.abs(off) <= 64
            bkt = _bucket(off * d)
            for h in range(8):
                bm[h, :, pi, ch, :] = np.where(ok, tdil[bkt, h], NEG)
    base, ntot = _dil_chunks()
    kv = np.zeros((128, ntot), np.float32)
    for pi, (_, d) in enumerate(PATS):
        nq = EXT // d
        nblk = -(-nq // 128)
        for c in range(d):
            for i in range(nblk + 1):
                dk = c + d * (128 * i - 64 + np.arange(128)) + 1024
                t = dk - 1152 + 2048 * core
                kv[:, base[(pi, c)] + i] = np.where((t >= 0) & (t < S), 0.0, NEG)
    return tz, cbv, bm, kv


def build_nc():
    nc = bass.Bass("TRN2", target_bir_lowering=False)
    sc = Sch(nc)

    def din(name, shape, dt=F32):
        return nc.dram_tensor(name, list(shape), dt, kind="ExternalInput").ap()

    def dscr(name, shape, dt=BF16):
        return nc.dram_tensor(name, list(shape), dt).ap()

    base, NCH = _dil_chunks()
    MISC = 512 + 256 + 8 + NFC * 3 + NFC + NET
    xr = din("xr", [S, DM])
    gains = din("gains", [3, 128, DM])
    wink = din("wink", [32, 128, 16 * 128])
    winv = din("winv", [2, 128, 16 * 512])
    winlv = din("winlv", [2, 128, 16 * 512])
    wout = din("wout", [128, 16 * DM])
    wgu = din("wgu", [88, 128, 16 * 128])
    wdn = din("wdn", [8, 128, NFC * 256])
    tzi = din("tz", [4, 128, 3 * TZW])
    cbi = din("cbv", [4, 128, 768])
    bmi = din("bm", [8, 128, 3 * 2 * 128])
    kvi = din("kv", [128, NCH])
    misc = din("misc", [128, MISC])
    identi = din("ident", [128, 128])
    y = nc.dram_tensor("y", [OWN, DM], F32, kind="ExternalOutput").ap()

    wink_b = dscr("wink_b", [32, 128, 16 * 128])
    winv_b = dscr("winv_b", [2, 128, 16 * 512])
    winlv_b = dscr("winlv_b", [2, 128, 16 * 512])
    wout_b = dscr("wout_b", [128, 16 * DM])
    wgu_b = dscr("wgu_b", [88, 128, 16 * 128])
    wdn_b = dscr("wdn_b", [8, 128, NFC * 256])
    hk = dscr("hk", [8, 128, S])
    hv = dscr("hv", [4, S, 256])
    qd = dscr("qd", [8, 128, EXT])
    lk = dscr("lk", [8, 128, DKN])
    lv = dscr("lv", [8, DKN + 2048, 128])
    ql = dscr("ql", [8, 128, EXT])
    odT = dscr("odT", [16, 128, EXT])
    x1s = dscr("x1s", [EXT, DM], F32)
    h2s = dscr("h2s", [16, 128, EXT + 2])

    R = Res
    pctr = {"n": 0}

    def psum_alloc(es_, nfp, tpcols):
        pctr["n"] += 1
        p_ = [es_.enter_context(nc.psum_tensor(f"pb{pctr['n']}_{i}", [128, 512], F32)) for i in range(nfp)]
        t_ = es_.enter_context(nc.psum_tensor(f"tp{pctr['n']}", [128, tpcols], BF16)) if tpcols else None
        return p_, [R() for _ in range(nfp)], t_, R()

    ident = nc.alloc_sbuf_tensor("ident_sb", [128, 128], BF16)
    misc_sb = nc.alloc_sbuf_tensor("misc_sb", [128, MISC], F32)
    neglam = nc.alloc_sbuf_tensor("neglam", [128, 1], F32)
    subg8 = nc.alloc_sbuf_tensor("subg8", [128, 256], F32)
    ones_bf = nc.alloc_sbuf_tensor("ones_bf", [128, 128], BF16)
    ones_f = nc.alloc_sbuf_tensor("ones_f", [128, 128], F32)
    zeros_bf = nc.alloc_sbuf_tensor("zeros_bf", [128, 16], BF16)
    sm = nc.alloc_sbuf_tensor("sm", [128, 64], F32)
    Rc = R()
    Rsm = [R() for _ in range(8)]
    O_LAM, O_SUBG, O_DILG, O_CW, O_CB, O_TM = 0, 512, 768, 776, 776 + NFC * 3, 776 + NFC * 4

    ds_w = sc.dsem()
    ds_c = sc.dsem()
    ds_ci = sc.dsem()
    sc.dma("pool", ident[:, :], identi[:, :], [], [Rc], ds_ci)
    sc.dma("sp", misc_sb[:, :], misc[:, :], [], [Rc], ds_c)
    sc.op("pool", lambda h: h.memset(ones_bf[:, :], 1.0), [], [Rc])
    sc.op("pool", lambda h: h.memset(ones_f[:, :], 1.0), [], [Rc])
    sc.op("pool", lambda h: h.memset(zeros_bf[:, :], 0.0), [], [Rc])
    lt = nc.alloc_sbuf_tensor("lt", [128, 2, 128], F32)
    sc.op("dve", lambda h: h.tensor_tensor(out=lt[:, 0, :], in0=misc_sb[:, 0:128], in1=misc_sb[:, 128:256], op=ALU.mult), [Rc], [Rsm[0]])
    sc.op("dve", lambda h: h.tensor_tensor(out=lt[:, 1, :], in0=misc_sb[:, 256:384], in1=misc_sb[:, 384:512], op=ALU.mult), [Rc], [Rsm[1]])
    sc.op("dve", lambda h: h.reduce_sum(out=sm[:, 0:1], in_=lt[:, 0, :], axis=AX.X), [Rsm[0]], [Rsm[2]])
    sc.op("dve", lambda h: h.reduce_sum(out=sm[:, 1:2], in_=lt[:, 1, :], axis=AX.X), [Rsm[1]], [Rsm[3]])
    sc.op("act", lambda h: h.activation(out=sm[:, 2:3], in_=sm[:, 0:1], func=AF.Exp), [Rsm[2]], [Rsm[4]])
    sc.op("act", lambda h: h.activation(out=sm[:, 3:4], in_=sm[:, 1:2], func=AF.Exp), [Rsm[3]], [Rsm[5]])
    sc.op("dve", lambda h: h.tensor_tensor(out=sm[:, 4:5], in0=sm[:, 3:4], in1=sm[:, 2:3], op=ALU.subtract), [Rsm[4], Rsm[5]], [Rsm[6]])
    sc.op("dve", lambda h: h.tensor_scalar(out=neglam[:, :], in0=sm[:, 4:5], scalar1=-0.2, scalar2=None, op0=ALU.add), [Rsm[6]], [Rc])
    sc.op("dve", lambda h: h.tensor_scalar(out=subg8[:, :], in0=misc_sb[:, O_SUBG:O_SUBG + 256], scalar1=0.8, scalar2=None, op0=ALU.mult), [Rc], [Rc])

    def cast_w(dst, src, n, width, R_list, sem_=None):
        for b in range(n):
            for c0 in range(0, width, 2048):
                c1 = min(width, c0 + 2048)
                sc.dma("pool", dst[b, :, c0:c1], src[b, :, c0:c1], [], [R_list[b]], sem_ or ds_w)

    _fam = [R(), R(), R(), R()]
    Rwink = [_fam[b_ // 8] for b_ in range(32)]
    Rwinv = [R()] * 2
    Rwinlv = [R()] * 2
    Rwout = [R()]
    Rwgu = [R() for _ in range(88)]
    Rwdn = [R() for _ in range(8)]
    fam_sem = [sc.dsem() for _ in range(6)]

    def cast_k(b0, b1):
        for b in range(b0, b1):
            sc.dma("pool", wink_b[b], wink[b], [], [Rwink[b]], fam_sem[b // 8])

    cast_k(8, 16)
    cast_w(winv_b, winv, 2, 16 * 512, Rwinv, fam_sem[4])
    cast_k(24, 32)
    cast_w(winlv_b, winlv, 2, 16 * 512, Rwinlv, fam_sem[5])
    cast_k(0, 8)
    cast_k(16, 24)

    def rstd_chain(ss_ap, out_ap, inv_n, eps, Rin, Rout, tmp):
        a, b = tmp
        sc.op("dve", lambda h: h.tensor_scalar(out=sm[:, a:a + 1], in0=ss_ap, scalar1=inv_n, scalar2=eps, op0=ALU.mult, op1=ALU.add), [Rin], [Rsm[6]])
        sc.op("act", lambda h: h.activation(out=sm[:, b:b + 1], in_=sm[:, a:a + 1], func=AF.Sqrt), [Rsm[6]], [Rsm[7]])
        sc.op("dve", lambda h: h.reciprocal(out=out_ap, in_=sm[:, b:b + 1]), [Rsm[7]], [Rout])

    ds_x = sc.dsem()
    ds_ld = sc.dsem()
    ds_st = sc.dsem()
    ds_st.is_store = True
    Rscr = {}

    def rs(key):
        if key not in Rscr:
            Rscr[key] = R()
        return Rscr[key]

    es = ExitStack()

    def A(name, shape, dt):
        return es.enter_context(nc.sbuf_tensor(name, shape, dt))

    pb, Rpb, tp, Rtp = psum_alloc(es, 6, 2048)
    P1 = {}
    P1["g1"] = A("g1", [128, DM], F32)
    Rg1 = R()
    sc.dma("sp", P1["g1"][:, :], gains[0], [], [Rg1], ds_c)
    hTb = [A(f"hT{i}", [128, 16, 1024], BF16) for i in range(2)]
    RhT = [[R() for _ in range(8)] for _ in range(2)]
    xtb = [A(f"xt{i}", [128, DM], F32) for i in range(2)]
    Rxt = [R(), R()]
    xnb = [A(f"xn{i}", [128, DM], BF16) for i in range(2)]
    Rxn = [R(), R()]
    junk = A("junk", [128, DM], BF16)
    Rjunk = R()
    ssb = A("ssb", [128, 4], F32)
    Rss = [R(), R()]
    Rrs = [R(), R()]
    wkt = [A(f"wkt{i}", [128, 16, 128], BF16) for i in range(8)]
    Rwkt = [R() for _ in range(8)]
    wvt = [A(f"wvt{i}", [128, 16, 512], BF16) for i in range(3)]
    Rwvt = [R() for _ in range(3)]
    stg = [A(f"stg{i}", [128, 512], BF16) for i in range(8)]
    Rstg = [R() for _ in range(8)]
    cnt = {"x": 0, "wk": 0, "wv": 0, "stg": 0, "pb": 0}

    def next_pb():
        i = cnt["pb"] % 6
        cnt["pb"] += 1
        return i

    def evac_store(psap, n_free_shape, dst_ap, Rp, Rdst, use_act):
        i = cnt["stg"] % 8
        cnt["stg"] += 1
        st_ap = n_free_shape(stg[i])
        if use_act:
            sc.op("act", lambda h: h.copy(out=st_ap, in_=psap), [Rp], [Rstg[i]])
        else:
            sc.op("dve", lambda h: h.tensor_copy(out=st_ap, in_=psap), [Rp], [Rstg[i]])
        sc.dma("pool", dst_ap, st_ap, [Rstg[i]], [Rdst], ds_st)

    def norm_tile(src_rows, xt, Rx, gain_t, Rg, xn, Rn, i2, extra_scale=None):
        sc.op("act", lambda h: h.activation(out=junk[:, :], in_=xt[:, :], func=AF.Square, accum_out=ssb[:, i2:i2 + 1]), [Rx], [Rjunk, Rss[i2]])
        rstd_chain(ssb[:, i2:i2 + 1], ssb[:, 2 + i2:3 + i2], 1.0 / DM, 1e-6, Rss[i2], Rrs[i2], (8, 9))
        if extra_scale is not None:
            sc.op("dve", lambda h: h.tensor_tensor(out=ssb[:, 2 + i2:3 + i2], in0=ssb[:, 2 + i2:3 + i2], in1=extra_scale, op=ALU.mult), [Rrs[i2], Rc], [Rrs[i2]])
        sc.op("dve", lambda h: h.scalar_tensor_tensor(out=xn[:, :], in0=xt[:, :], scalar=ssb[:, 2 + i2:3 + i2], in1=gain_t[:, :], op0=ALU.mult, op1=ALU.mult), [Rx, Rrs[i2], Rg], [Rn])

    def transpose16(xn, Rn, dst_ap, Rdst_list):
        for k in range(16):
            sc.op("pe", lambda h, k=k: h.transpose(out=tp[:, k * 128:(k + 1) * 128], in_=xn[:, k * 128:(k + 1) * 128], identity=ident[:, :]),
                  [Rn, Rc], [Rtp], defer=(k < 15))
        sc.op("act", lambda h: h.copy(out=dst_ap, in_=tp[:, :].rearrange("p (k c) -> p k c", k=16)), [Rtp], Rdst_list)

    def tile_groups(ta, tb):
        g = []
        t = ta
        while t < tb:
            e = min(tb, t + 4)
            g.append((t, e))
            t = e
        return g

    def dkcol(T):
        return (T - 119) * 128 if T >= 119 else (T + 9) * 128

    def ecol(T):
        return 0 if T == 127 else (T + 1) * 128

    def norm_items(ST, hb):
        hT = hTb[hb]
        items = []
        for t in range(8):
            st_ = {}

            def f1(t=t, st_=st_):
                T = ST * 8 + t
                i2 = cnt["x"] % 2
                cnt["x"] += 1
                st_["i2"] = i2
                sc.dma("sp", xtb[i2][:, :], xr[T * 128:(T + 1) * 128, :], [], [Rxt[i2]], ds_x)
                norm_tile(None, xtb[i2], Rxt[i2], P1["g1"], Rg1, xnb[i2], Rxn[i2], i2)

            def f2(t=t, st_=st_):
                i2 = st_["i2"]
                transpose16(xnb[i2], Rxn[i2], hT[:, :, t * 128:(t + 1) * 128], [RhT[hb][t]])
            items.append(f1)
            items.append(f2)
        out = [items[0]]
        for t in range(1, 8):
            out.append(items[2 * t])
            out.append(items[2 * t - 1])
        out.append(items[15])
        return out

    def mm_items(ST, hb):
        hT = hTb[hb]
        items = []

        def ktype(blk, dst_fn, ta, tb):
            st_ = {}

            def ld():
                i = cnt["wk"] % 8
                cnt["wk"] += 1
                st_["i"] = i
                sc.dma("sp", wkt[i][:, :, :].rearrange("p k c -> p (k c)"), wink_b[blk], [Rwink[blk]], [Rwkt[i]], ds_ld)
            items.append(ld)
            for (a, b) in tile_groups(ta, tb):
                def grp(a=a, b=b):
                    i = st_["i"]
                    n = (b - a) * 128
                    pi = next_pb()
                    for k in range(16):
                        sc.op("pe", lambda h, k=k: h.matmul(pb[pi][:, 0:n], lhsT=wkt[i][:, k, :], rhs=hT[:, k, a * 128:b * 128], start=(k == 0), stop=(k == 15)),
                              [Rwkt[i]] + RhT[hb][a:b], [Rpb[pi]], defer=(k < 15))
                    dst, Rd = dst_fn(a, b)
                    evac_store(pb[pi][:, 0:n], lambda s_: s_[:, 0:n], dst, Rpb[pi], Rd, use_act=(cnt["stg"] % 2 == 0))
                items.append(grp)

        def vtype(wsrc, Rw, tiles, dst_fn, Rd):
            st_ = {}

            def ld():
                i = cnt["wv"] % 3
                cnt["wv"] += 1
                st_["i"] = i
                sc.dma("sp", wvt[i][:, :, :].rearrange("p k c -> p (k c)"), wsrc, [Rw], [Rwvt[i]], ds_ld)
            items.append(ld)
            for t in tiles:
                def tl(t=t):
                    i = st_["i"]
                    pi = next_pb()
                    for k in range(16):
                        sc.op("pe", lambda h, k=k: h.matmul(pb[pi][:, 0:512], lhsT=hT[:, k, t * 128:(t + 1) * 128], rhs=wvt[i][:, k, :], start=(k == 0), stop=(k == 15)),
                              [Rwvt[i], RhT[hb][t]], [Rpb[pi]], defer=(k < 15))
                    psap, shp, dst = dst_fn(ST * 8 + t, pb[pi])
                    evac_store(psap, shp, dst, Rpb[pi], Rd, use_act=(cnt["stg"] % 2 == 0))
                items.append(tl)

        for b8 in range(8):
            ktype(8 + b8, lambda a, b, b8=b8: (hk[b8, :, (ST * 8 + a) * 128:(ST * 8 + b) * 128], rs(("hk", b8))), 0, 8)

        def vdst(dram, nh, h0):
            return lambda T, p_: (p_[:, 0:512].rearrange("p (h c) -> p h c", h=nh), (lambda s_: s_[:, 0:512].rearrange("p (h c) -> p h c", h=nh)),
                                  dram(T)[h0:h0 + nh].rearrange("h r c -> r h c"))

        for u in range(2):
            vtype(winv_b[u], Rwinv[u], list(range(8)), vdst(lambda T: hv[:, T * 128:(T + 1) * 128, :], 2, 2 * u), rs(("hv", u)))
        dil_tiles = [t for t in range(8) if (ST * 8 + t) >= 119 or (ST * 8 + t) <= 24]
        q_tiles = [t for t in range(8) if (ST * 8 + t) == 127 or (ST * 8 + t) <= 16]
        if dil_tiles:
            ta, tb = dil_tiles[0], dil_tiles[-1] + 1
            for b8 in range(8):
                ktype(24 + b8, lambda a, b, b8=b8: (lk[b8, :, dkcol(ST * 8 + a):dkcol(ST * 8 + a) + (b - a) * 128], rs(("lk", b8))), ta, tb)
            for jj in range(2):
                vtype(winlv_b[jj], Rwinlv[jj], dil_tiles, vdst(lambda T: lv[:, dkcol(T):dkcol(T) + 128, :], 4, 4 * jj), rs(("lv", jj)))
        if q_tiles:
            ta, tb = q_tiles[0], q_tiles[-1] + 1
            for b8 in range(8):
                ktype(b8, lambda a, b, b8=b8: (qd[b8, :, ecol(ST * 8 + a):ecol(ST * 8 + a) + (b - a) * 128], rs(("qd", b8))), ta, tb)
                ktype(16 + b8, lambda a, b, b8=b8: (ql[b8, :, ecol(ST * 8 + a):ecol(ST * 8 + a) + (b - a) * 128], rs(("ql", b8))), ta, tb)
        return items

    NST = int(os.environ.get("NST", "16"))
    order = [st_ for st_ in range(NST) if 2 <= st_ <= 14] + [st_ for st_ in range(NST) if not (2 <= st_ <= 14)]
    for f in norm_items(order[0], 0):
        f()
    for oi_, ST in enumerate(order):
        mm = mm_items(ST, oi_ % 2)
        nx = norm_items(order[oi_ + 1], (oi_ + 1) % 2) if oi_ + 1 < len(order) else []
        step = max(1, len(mm) // (len(nx) + 1)) if nx else len(mm) + 1
        ni = 0
        for idx, it in enumerate(mm):
            it()
            if nx and (idx + 1) % step == 0 and ni < len(nx):
                nx[ni]()
                ni += 1
        while ni < len(nx):
            nx[ni]()
            ni += 1

    sc.barrier()
    es.close()

    es = ExitStack()
    pb, Rpb, tp, Rtp = psum_alloc(es, 7, 1024)
    cast_jobs = []
    for c0 in range(0, 16 * DM, 2048):
        cast_jobs.append((wout_b[:, c0:c0 + 2048], wout[:, c0:c0 + 2048], Rwout[0]))
    for b_ in range(88):
        cast_jobs.append((wgu_b[b_], wgu[b_], Rwgu[b_]))
    for b_ in range(8):
        for c0 in range(0, NFC * 256, 1024):
            cast_jobs.append((wdn_b[b_, :, c0:c0 + 1024], wdn[b_, :, c0:c0 + 1024], Rwdn[b_]))
    cast_pos = {"i": 0}

    def cast_some(nj, dep):
        for _ in range(nj):
            if cast_pos["i"] >= len(cast_jobs):
                return
            d_, s_, r_ = cast_jobs[cast_pos["i"]]
            cast_pos["i"] += 1
            sc.prewait("pool", dep, [])
            sc.dma("pool", d_, s_, [], [r_], ds_w)
    KT = [A("KT0", [128, S], BF16), A("KT1", [128, S], BF16)]
    RKT = [[R() for _ in range(8)] for _ in range(2)]
    vt = A("vt", [128, 128, 257], BF16)
    Rvt = [R() for _ in range(8)]
    QT = [A("QT0", [128, EXT], BF16), A("QT1", [128, EXT], BF16)]
    RQT = [R(), R()]
    tz_sb = A("tz_sb", [128, 3, TZW], F32)
    cb_sb = A("cb_sb", [128, 768], F32)
    Rtz = R()
    pT = [A(f"pT{i}", [128, 384], BF16) for i in range(5)]
    RpT = [R() for _ in range(5)]
    sbt = [A(f"sbt{i}", [128, 384], F32) for i in range(3)]
    Rsbt = [R() for _ in range(3)]
    Amap = [A(f"Amap{i}", [128, 3, 256], F32) for i in range(2)]
    RA = [[R() for _ in range(3)] for _ in range(2)]
    ot = A("ot", [128, 256], F32)
    Rot = R()
    onb = A("onb", [128, 256], BF16)
    Ronb = R()
    ost = [A(f"ost{i}", [128, 2, 128], BF16) for i in range(2)]
    Rost = [R(), R()]
    junk2 = A("junk2", [128, 256], BF16)
    Rj2 = R()
    sm2 = A("sm2", [128, 16], F32)
    Rs2 = [R() for _ in range(8)]
    sc.op("pool", lambda h: h.memset(vt[:, :, 256:257], 1.0), [], Rvt)
    c2 = {"s": 0, "p": 0, "sb": 0, "o": 0}
    for hh in range(4):
        for m in range(2):
            for g8 in range(8):
                sc.dma("sp", KT[m][:, g8 * 2048:(g8 + 1) * 2048], hk[2 * hh + m, :, g8 * 2048:(g8 + 1) * 2048], [rs(("hk", 2 * hh + m))], [RKT[m][g8]], ds_ld)
            sc.dma("sp", QT[m][:, :], qd[2 * hh + m], [rs(("qd", 2 * hh + m))], [RQT[m]], ds_ld)
        for g8 in range(8):
            sc.dma("sp", vt[:, g8 * 16:(g8 + 1) * 16, 0:256], hv[hh, g8 * 2048:(g8 + 1) * 2048, :].rearrange("(kb p) c -> p kb c", p=128),
                   [rs(("hv", hh // 2))], [Rvt[g8]], ds_ld)
        sc.dma("sp", tz_sb[:, :, :].rearrange("p a b -> p (a b)"), tzi[hh], [], [Rtz], ds_ld)
        sc.dma("sp", cb_sb[:, :], cbi[hh], [], [Rtz], ds_ld)
        for qb in range(6):
            if hh > 0 or qb > 0:
                cast_some(9, [Rot])
            for m in range(2):
                qs = QT[m][:, qb * 384:(qb + 1) * 384]

                def qk(kb):
                    si = 3 + (c2["s"] % 4)
                    c2["s"] += 1
                    sc.op("pe", lambda h: h.matmul(pb[si][:, 0:384], lhsT=KT[m][:, kb * 128:(kb + 1) * 128], rhs=qs, start=True, stop=True),
                          [RKT[m][kb // 16], RQT[m]], [Rpb[si]])
                    return si

                def expo(kb, si):
                    pi_ = c2["p"] % 5
                    c2["p"] += 1
                    var, st = _diff_tile_kind(kb, qb)
                    if var is None:
                        sc.op("act", lambda h: h.activation(out=pT[pi_][:, :], in_=pb[si][:, 0:384], func=AF.Exp,
                                                            bias=cb_sb[:, qb * 128 + kb:qb * 128 + kb + 1], scale=SCALE),
                              [Rpb[si], Rtz], [RpT[pi_]])
                    else:
                        bi = c2["sb"] % 3
                        c2["sb"] += 1
                        sc.op("dve", lambda h: h.scalar_tensor_tensor(out=sbt[bi][:, :], in0=pb[si][:, 0:384], scalar=SCALE,
                                                                      in1=tz_sb[:, var, st:st + 384], op0=ALU.mult, op1=ALU.add),
                              [Rpb[si], Rtz], [Rsbt[bi]])
                        sc.op("act", lambda h: h.activation(out=pT[pi_][:, :], in_=sbt[bi][:, :], func=AF.Exp), [Rsbt[bi]], [RpT[pi_]])
                    return pi_

                def pv(kb, pi_):
                    for j in range(3):
                        sc.op("pe", lambda h, j=j: h.matmul(pb[j][:, 0:257], lhsT=pT[pi_][:, j * 128:(j + 1) * 128], rhs=vt[:, kb, :],
                                                           start=(kb == 0), stop=(kb == 127)),
                              [RpT[pi_], Rvt[kb // 16]], [Rpb[j]], defer=(j < 2))

                sidx = {}
                for i2_ in range(64 + 2):
                    done = []
                    for kb in (2 * i2_ - 4, 2 * i2_ - 3):
                        if 0 <= kb < 128:
                            done.append((kb, expo(kb, sidx[kb])))
                    if done:
                        sc.prewait("pe", [RpT[p__] for _, p__ in done], [Rpb[0], Rpb[1], Rpb[2]])
                    for kb in (2 * i2_, 2 * i2_ + 1):
                        if kb < 128:
                            sidx[kb] = qk(kb)
                    for kb, p__ in done:
                        pv(kb, p__)
                for j in range(3):
                    sc.op("dve", lambda h, j=j: h.reciprocal(out=sm2[:, j:j + 1], in_=pb[j][:, 256:257]), [Rpb[j]], [Rs2[j]])
                    sc.op("dve", lambda h, j=j: h.tensor_scalar(out=Amap[m][:, j, :], in0=pb[j][:, 0:256], scalar1=sm2[:, j:j + 1], scalar2=None, op0=ALU.mult),
                          [Rpb[j], Rs2[j]], [RA[m][j]])
            for j in range(3):
                et = qb * 3 + j
                sc.op("dve", lambda h, j=j: h.scalar_tensor_tensor(out=ot[:, :], in0=Amap[1][:, j, :], scalar=neglam[:, 0:1], in1=Amap[0][:, j, :],
                                                                  op0=ALU.mult, op1=ALU.add), [RA[0][j], RA[1][j], Rc], [Rot])
                sc.op("act", lambda h: h.activation(out=junk2[:, :], in_=ot[:, :], func=AF.Square, accum_out=sm2[:, 4:5]), [Rot], [Rj2, Rs2[4]])
                sc.op("dve", lambda h: h.tensor_scalar(out=sm2[:, 5:6], in0=sm2[:, 4:5], scalar1=1.0 / 256, scalar2=1e-5, op0=ALU.mult, op1=ALU.add), [Rs2[4]], [Rs2[5]])
                sc.op("act", lambda h: h.activation(out=sm2[:, 6:7], in_=sm2[:, 5:6], func=AF.Sqrt), [Rs2[5]], [Rs2[6]])
                sc.op("dve", lambda h: h.reciprocal(out=sm2[:, 7:8], in_=sm2[:, 6:7]), [Rs2[6]], [Rs2[7]])
                sc.op("dve", lambda h: h.scalar_tensor_tensor(out=onb[:, :], in0=ot[:, :], scalar=sm2[:, 7:8], in1=subg8[:, :], op0=ALU.mult, op1=ALU.mult),
                      [Rot, Rs2[7], Rc], [Ronb])
                for c_ in range(2):
                    sc.op("pe", lambda h, c_=c_: h.transpose(out=tp[:, c_ * 128:(c_ + 1) * 128], in_=onb[:, c_ * 128:(c_ + 1) * 128], identity=ident[:, :]),
                          [Ronb, Rc], [Rtp], defer=(c_ < 1))
                oi = c2["o"] % 2
                c2["o"] += 1
                sc.op("act", lambda h: h.copy(out=ost[oi][:, :, :], in_=tp[:, 0:256].rearrange("p (c t) -> p c t", c=2)), [Rtp], [Rost[oi]])
                sc.dma("pool", odT[2 * hh:2 * hh + 2, :, et * 128:(et + 1) * 128].rearrange("c p t -> p c t"), ost[oi][:, :, :], [Rost[oi]], [rs(("od", et))], ds_st)
    cast_some(10 ** 6, [])
    sc.barrier()
    es.close()

    es = ExitStack()
    pb, Rpb, tp, Rtp = psum_alloc(es, 8, 0)
    KlTb = [A(f"KlT{i}", [128, DKN], BF16) for i in range(2)]
    QlTb = [A(f"QlT{i}", [128, EXT], BF16) for i in range(2)]
    RKlb, RQlb = [R(), R()], [R(), R()]
    bm_sbb = [A(f"bm_sb{i}", [128, 3, 2, 128], F32) for i in range(2)]
    kv_sb = A("kv_sb", [128, NCH], F32)
    Rbmb, Rkv = [R(), R()], R()
    vchb = [A(f"vch{i}", [128, NCH, 128], BF16) for i in range(2)]
    Rvchb = [R(), R()]
    accUD = A("accUD", [128, 2, EXT], F32)
    accU = accUD[:, 0, :]
    accD = accUD[:, 1, :]
    RaU = R()
    RaD = RaU
    Rss_ = [R() for _ in range(8)]
    Rso_ = [R() for _ in range(8)]
    sq = A("sq", [128, EXT], F32)
    Rsq = R()
    rst = A("rst", [128, EXT], F32)
    Rrst = R()
    sb3 = [A(f"sb3{i}", [128, 2, 128], F32) for i in range(6)]
    Rsb3 = [R() for _ in range(6)]
    pT3 = [A(f"pT3{i}", [128, 2, 128], BF16) for i in range(6)]
    RpT3 = [R() for _ in range(6)]
    olb = [A(f"olb{i}", [128, 512], BF16) for i in range(2)]
    Rolb = [R(), R()]
    sc.dma("sp", kv_sb[:, :], kvi[:, :], [], [Rkv], ds_ld)
    c3 = {"b": 0, "o": 0}

    def dil_loads(hl):
        hb = hl % 2
        sc.dma("sp", KlTb[hb][:, :], lk[hl], [rs(("lk", hl))], [RKlb[hb]], ds_ld)
        sc.dma("sp", QlTb[hb][:, :], ql[hl], [rs(("ql", hl))], [RQlb[hb]], ds_ld)
        sc.dma("sp", bm_sbb[hb][:, :, :, :].rearrange("p a b c -> p (a b c)"), bmi[hl], [], [Rbmb[hb]], ds_ld)
        for p_, (_, d) in enumerate(PATS):
            nq = EXT // d
            nblk = -(-nq // 128)
            nch = nblk + 1
            for c in range(d):
                g0 = base[(p_, c)]
                row0 = c - 64 * d + 1024
                src = lv[hl, row0:row0 + d * 128 * nch:d, :].rearrange("(i j) c -> j i c", j=128)
                sc.dma("sp", vchb[hb][:, g0:g0 + nch, :], src, [rs(("lv", hl // 4))], [Rvchb[hb]], ds_ld)

    dil_loads(0)
    for hl in range(8):
        hb = hl % 2
        KlT, QlT, RKl, RQl = KlTb[hb], QlTb[hb], RKlb[hb], RQlb[hb]
        bm_sb, Rbm, vch, Rvch = bm_sbb[hb], Rbmb[hb], vchb[hb], Rvchb[hb]
        if hl + 1 < 8:
            dil_loads(hl + 1)
        sc.op("pool", lambda h: h.memset(accUD[:, :, :], 0.0), [], [RaU])
        blocks = []
        for p_, (_, d) in enumerate(PATS):
            nq = EXT // d
            nblk = -(-nq // 128)
            for c in range(d):
                for blk in range(nblk):
                    blocks.append((p_, d, c, blk, nq))

        def stage1(B):
            p_, d, c, blk, nq = B
            u0 = blk * 128
            n = min(128, nq - u0)
            kA = c + d * (u0 - 64) + 1024
            kB = kA + 128 * d
            q0 = c + d * u0
            qs = QlT[:, q0:q0 + d * (n - 1) + 1:d]
            j = c3["b"] % 4
            bi = c3["b"] % 6
            c3["b"] += 1
            ps_s = pb[j][:, 0:256]
            sc.op("pe", lambda h: h.matmul(ps_s[:, 0:n], lhsT=KlT[:, kA:kA + d * 127 + 1:d], rhs=qs, start=True, stop=True),
                  [RKl, RQl], [Rss_[j]], defer=True)
            sc.op("pe", lambda h: h.matmul(ps_s[0:n, 128:128 + n], lhsT=KlT[:, kB:kB + d * (n - 1) + 1:d], rhs=qs, start=True, stop=True),
                  [RKl, RQl], [Rss_[j]])
            return (bi, n, q0, j)

        def stage2(B, st):
            p_, d, c, blk, nq = B
            bi, n, q0, j = st
            ps_s = pb[j][:, 0:256]
            gi = base[(p_, c)] + blk
            if n == 128:
                sc.op("dve", lambda h: h.scalar_tensor_tensor(out=sb3[bi][:, :, :].rearrange("p a b -> p (a b)"), in0=ps_s[:, 0:256], scalar=SCALE,
                                                              in1=bm_sb[:, p_, :, :].rearrange("p a b -> p (a b)"), op0=ALU.mult, op1=ALU.add),
                      [Rss_[j], Rbm], [Rsb3[bi]])
            else:
                sc.op("dve", lambda h: h.scalar_tensor_tensor(out=sb3[bi][:, 0, 0:n], in0=ps_s[:, 0:n], scalar=SCALE, in1=bm_sb[:, p_, 0, 0:n],
                                                              op0=ALU.mult, op1=ALU.add), [Rss_[j], Rbm], [Rsb3[bi]])
                sc.op("dve", lambda h: h.scalar_tensor_tensor(out=sb3[bi][0:n, 1, 0:n], in0=ps_s[0:n, 128:128 + n], scalar=SCALE, in1=bm_sb[0:n, p_, 1, 0:n],
                                                              op0=ALU.mult, op1=ALU.add), [Rss_[j], Rbm], [Rsb3[bi]])
            sc.op("act", lambda h: h.activation(out=pT3[bi][:, 0, 0:n], in_=sb3[bi][:, 0, 0:n], func=AF.Exp, bias=kv_sb[:, gi:gi + 1]),
                  [Rsb3[bi], Rkv], [RpT3[bi]])
            sc.op("act", lambda h: h.activation(out=pT3[bi][0:n, 1, 0:n], in_=sb3[bi][0:n, 1, 0:n], func=AF.Exp, bias=kv_sb[0:n, gi + 1:gi + 2]),
                  [Rsb3[bi], Rkv], [RpT3[bi]])

        def stage3(B, st):
            p_, d, c, blk, nq = B
            bi, n, q0, j = st
            ps_o = pb[4 + j][:, 0:256]
            gi = base[(p_, c)] + blk
            sc.op("pe", lambda h: h.matmul(ps_o[:, 0:n], lhsT=vch[:, gi, :], rhs=pT3[bi][:, 0, 0:n], start=True, stop=False),
                  [Rvch, RpT3[bi]], [Rso_[j]], defer=True)
            sc.op("pe", lambda h: h.matmul(ps_o[:, 0:n], lhsT=vch[0:n, gi + 1, :], rhs=pT3[bi][0:n, 1, 0:n], start=False, stop=True),
                  [Rvch, RpT3[bi]], [Rso_[j]], defer=True)
            sc.op("pe", lambda h: h.matmul(ps_o[:, 128:128 + n], lhsT=ones_bf[:, :], rhs=pT3[bi][:, 0, 0:n], start=True, stop=False),
                  [Rc, RpT3[bi]], [Rso_[j]], defer=True)
            sc.op("pe", lambda h: h.matmul(ps_o[:, 128:128 + n], lhsT=ones_bf[0:n, :], rhs=pT3[bi][0:n, 1, 0:n], start=False, stop=True),
                  [Rc, RpT3[bi]], [Rso_[j]])
            esl = slice(q0, q0 + d * (n - 1) + 1, d)
            sc.op("dve", lambda h: h.tensor_tensor(out=accUD[:, :, esl], in0=accUD[:, :, esl], in1=ps_o.rearrange("p (a b) -> p a b", a=2)[:, :, 0:n], op=ALU.add),
                  [Rso_[j], RaU], [RaU])

        nb_ = len(blocks)
        sts = {}
        for i in range(nb_ + 3):
            if i < nb_:
                sts[i] = stage1(blocks[i])
            if 0 <= i - 2 < nb_:
                stage2(blocks[i - 2], sts[i - 2])
            if 0 <= i - 3 < nb_:
                stage3(blocks[i - 3], sts[i - 3])
        sc.op("dve", lambda h: h.tensor_scalar(out=accUD[:, 1, :], in0=accUD[:, 1, :], scalar1=1e-18, scalar2=None, op0=ALU.max), [RaD], [RaD])
        sc.op("act", lambda h: h.activation(out=accUD[:, 1, :], in_=accUD[:, 1, :], func=AF.Ln), [RaD], [RaD])
        sc.op("act", lambda h: h.activation(out=accUD[:, 1, :], in_=accUD[:, 1, :], func=AF.Exp, scale=-1.0), [RaD], [RaD])
        sc.op("dve", lambda h: h.tensor_tensor(out=accUD[:, 0, :], in0=accUD[:, 0, :], in1=accUD[:, 1, :], op=ALU.mult), [RaU, RaD], [RaU])
        sc.op("pool", lambda h: h.tensor_tensor(out=sq[:, :], in0=accUD[:, 0, :], in1=accUD[:, 0, :], op=ALU.mult), [RaU], [Rsq])
        for c0 in range(0, EXT, 512):
            n = min(512, EXT - c0)
            pi = c3["o"] % 4
            sc.op("pe", lambda h: h.matmul(pb[pi][:, 0:n], lhsT=ones_f[:, :], rhs=sq[:, c0:c0 + n], start=True, stop=True), [Rc, Rsq], [Rss_[pi]])
            sc.op("dve", lambda h: h.tensor_scalar(out=rst[:, c0:c0 + n], in0=pb[pi][:, 0:n], scalar1=1.0 / 128, scalar2=1e-6, op0=ALU.mult, op1=ALU.add),
                  [Rss_[pi]], [Rrst])
            sc.op("act", lambda h: h.activation(out=rst[:, c0:c0 + n], in_=rst[:, c0:c0 + n], func=AF.Ln), [Rrst], [Rrst])
            sc.op("act", lambda h: h.activation(out=rst[:, c0:c0 + n], in_=rst[:, c0:c0 + n], func=AF.Exp, scale=-0.5), [Rrst], [Rrst])
            oi = c3["o"] % 2
            c3["o"] += 1
            sc.op("dve", lambda h: h.scalar_tensor_tensor(out=olb[oi][:, 0:n], in0=accUD[:, 0, c0:c0 + n], scalar=misc_sb[:, O_DILG + hl:O_DILG + hl + 1],
                                                          in1=rst[:, c0:c0 + n], op0=ALU.mult, op1=ALU.mult), [RaU, Rrst, Rc], [Rolb[oi]])
            sc.dma("pool", odT[8 + hl, :, c0:c0 + n], olb[oi][:, 0:n], [Rolb[oi]], [rs(("ol", hl))], ds_st)
    sc.barrier()
    es.close()

    es = ExitStack()
    pb, Rpb, tp, Rtp = psum_alloc(es, 6, 2048)
    Wo = A("Wo", [128, 16, DM], BF16)
    RWo = R()
    g2 = A("g2", [128, DM], F32)
    gf = A("gf", [128, DM], F32)
    Rg2 = R()
    sc.dma("sp", Wo[:, :, :].rearrange("p k c -> p (k c)"), wout_b[:, :], [Rwout[0]], [RWo], ds_ld)
    sc.dma("sp", g2[:, :], gains[1], [], [Rg2], ds_c)
    sc.dma("sp", gf[:, :], gains[2], [], [Rg2], ds_c)
    Rh2 = R()
    sc.dma("pool", h2s[:, :, 0:1].rearrange("k p o -> p k o"), zeros_bf[:, 0:16].rearrange("p (k o) -> p k o", o=1), [Rc], [Rh2], ds_st, slow=True)
    sc.dma("pool", h2s[:, :, EXT + 1:EXT + 2].rearrange("k p o -> p k o"), zeros_bf[:, 0:16].rearrange("p (k o) -> p k o", o=1), [Rc], [Rh2], ds_st, slow=True)
    OTb = [A(f"OT{i}", [128, 16, 128], BF16) for i in range(2)]
    ROT = [R(), R()]
    xt4 = [A(f"xt4{i}", [128, DM], F32) for i in range(2)]
    Rxt4 = [R(), R()]
    x1b = [A(f"x1b{i}", [128, DM], F32) for i in range(2)]
    Rx1 = [R(), R()]
    h2b = [A(f"h2b{i}", [128, DM], BF16) for i in range(2)]
    Rh2b = [R(), R()]
    hst = [A(f"hst{i}", [128, 16, 128], BF16) for i in range(2)]
    Rhst = [R(), R()]
    junk4 = A("junk4", [128, DM], BF16)
    ssb4 = A("ssb4", [128, 4], F32)
    junk, Rjunk, ssb = junk4, R(), ssb4
    Rss[0], Rss[1], Rrs[0], Rrs[1] = R(), R(), R(), R()
    Rx1s = [R() for _ in range(NET)]
    for et in range(NET):
        i2 = et % 2
        r0 = (S - 128) if et == 0 else (et - 1) * 128
        od_deps = [rs(("od", et))] + [rs(("ol", h_)) for h_ in range(8)]
        sc.dma("sp", OTb[i2][:, :, :], odT[:, :, et * 128:(et + 1) * 128].rearrange("c p t -> p c t"), od_deps, [ROT[i2]], ds_ld)
        sc.dma("sp", xt4[i2][:, :], xr[r0:r0 + 128, :], [], [Rxt4[i2]], ds_x)
        for db in range(4):
            for ci in range(16):
                sc.op("pe", lambda h, ci=ci: h.matmul(pb[db][:, :], lhsT=OTb[i2][:, ci, :], rhs=Wo[:, ci, db * 512:(db + 1) * 512], start=(ci == 0), stop=(ci == 15)),
                      [ROT[i2], RWo], [Rpb[db]], defer=(ci < 15))
            sc.op("dve", lambda h, db=db: h.tensor_tensor(out=x1b[i2][:, db * 512:(db + 1) * 512], in0=pb[db][:, :], in1=xt4[i2][:, db * 512:(db + 1) * 512], op=ALU.add),
                  [Rpb[db], Rxt4[i2]], [Rx1[i2]])
        sc.dma("pool", x1s[et * 128:(et + 1) * 128, :], x1b[i2][:, :], [Rx1[i2]], [Rx1s[et]], ds_st)
        norm_tile(None, x1b[i2], Rx1[i2], g2, Rg2, h2b[i2], Rh2b[i2], i2, extra_scale=misc_sb[:, O_TM + et:O_TM + et + 1])
        transpose16(h2b[i2], Rh2b[i2], hst[i2][:, :, :], [Rhst[i2]])
        sc.dma("pool", h2s[:, :, 1 + et * 128:1 + (et + 1) * 128].rearrange("k p t -> p k t"), hst[i2][:, :, :], [Rhst[i2]], [Rh2], ds_st)
    sc.barrier()
    es.close()

    es = ExitStack()
    pb, Rpb, tp, Rtp = psum_alloc(es, 8, 0)
    gf = A("gfb", [128, DM], F32)
    Rgf = R()
    sc.dma("sp", gf[:, :], gains[2], [], [Rgf], ds_c)
    h2g = [A(f"h2g{i}", [128, 16, 514], BF16) for i in range(1)] * 2
    Rh2g = [R()] * 2
    x1g = [A(f"x1g{i}", [128, 4, DM], F32) for i in range(1)] * 2
    Rx1g = [[R() for _ in range(4)]] * 2
    wgt = [A(f"wgt{i}", [128, 16, 128], BF16) for i in range(3)]
    wut = [A(f"wut{i}", [128, 16, 128], BF16) for i in range(3)]
    Rwgt = [R() for _ in range(3)]
    Rwut = [R() for _ in range(3)]
    aT = A("aT", [128, NFC, 512], BF16)
    RaT = [R() for _ in range(NFC)]
    wds = [A(f"wds{i}", [128, NFC, 256], BF16) for i in range(2)]
    Rwds = [R(), R()]
    ca = [A(f"ca{i}", [128, 512], F32) for i in range(4)]
    Rca = [R() for _ in range(4)]
    yo = [A(f"yo{i}", [128, DM], F32) for i in range(1)] * 2
    Ryo = [R()] * 2
    junk5 = A("junk5", [128, DM], BF16)
    Rj5 = R()
    sm5 = A("sm5", [128, 8], F32)
    Rs5 = [R() for _ in range(4)]
    Ry = R()
    Rhalo = [R(), R()]
    c4 = {"w": 0, "d": 0, "y": 0}
    CW, CB = O_CW, O_CB
    for g in range(4):
        gi2 = g % 2
        e0 = 128 + 512 * g
        sc.dma("sp", h2g[gi2][:, :, :], h2s[:, :, e0:e0 + 514].rearrange("k p t -> p k t"), [Rh2], [Rh2g[gi2]], ds_ld)
        for tt in range(4):
            et = 1 + g * 4 + tt
            sc.dma("sp", x1g[gi2][:, tt, :], x1s[et * 128:(et + 1) * 128, :], [Rx1s[et]], [Rx1g[gi2][tt]], ds_x)
        for fc in range(NFC):
            wi = c4["w"] % 3
            c4["w"] += 1
            sc.dma("sp", wgt[wi][:, :, :].rearrange("p k c -> p (k c)"), wgu_b[fc], [Rwgu[fc]], [Rwgt[wi]], ds_ld)
            sc.dma("sp", wut[wi][:, :, :].rearrange("p k c -> p (k c)"), wgu_b[NFC + fc], [Rwgu[NFC + fc]], [Rwut[wi]], ds_ld)
            pg = fc % 2
            pu = 2 + fc % 2
            hi = fc % 2
            ph = pb[6 + hi][:, 0:2]
            for k in range(16):
                sc.op("pe", lambda h, k=k: h.matmul(pb[pg][:, :], lhsT=wgt[wi][:, k, :], rhs=h2g[gi2][:, k, 1:513], start=(k == 0), stop=(k == 15)),
                      [Rwgt[wi], Rh2g[gi2]], [Rpb[pg]], defer=True)
                sc.op("pe", lambda h, k=k: h.matmul(ph, lhsT=wgt[wi][:, k, :], rhs=h2g[gi2][:, k, 0:514:513], start=(k == 0), stop=(k == 15)),
                      [Rwgt[wi], Rh2g[gi2]], [Rhalo[hi]], defer=(k < 15))
            for k in range(16):
                sc.op("pe", lambda h, k=k: h.matmul(pb[pu][:, :], lhsT=wut[wi][:, k, :], rhs=h2g[gi2][:, k, 1:513], start=(k == 0), stop=(k == 15)),
                      [Rwut[wi], Rh2g[gi2]], [Rpb[pu]], defer=(k < 15))
            cw0 = misc_sb[:, CW + fc * 3 + 0:CW + fc * 3 + 1]
            cw1 = misc_sb[:, CW + fc * 3 + 1:CW + fc * 3 + 2]
            cw2 = misc_sb[:, CW + fc * 3 + 2:CW + fc * 3 + 3]
            cbb = misc_sb[:, CB + fc:CB + fc + 1]
            sc.op("act", lambda h: h.activation(out=ca[0][:, :], in_=pb[pg][:, :], func=AF.Identity, bias=cbb, scale=cw1), [Rpb[pg], Rc], [Rca[0]])
            sc.op("dve", lambda h: h.scalar_tensor_tensor(out=ca[1][:, 1:512], in0=pb[pg][:, 0:511], scalar=cw0, in1=ca[0][:, 1:512], op0=ALU.mult, op1=ALU.add),
                  [Rpb[pg], Rca[0], Rc], [Rca[1]])
            sc.op("dve", lambda h: h.scalar_tensor_tensor(out=ca[1][:, 0:1], in0=ph[:, 0:1], scalar=cw0, in1=ca[0][:, 0:1], op0=ALU.mult, op1=ALU.add),
                  [Rhalo[hi], Rca[0], Rc], [Rca[1]])
            sc.op("dve", lambda h: h.scalar_tensor_tensor(out=ca[2][:, 0:511], in0=pb[pg][:, 1:512], scalar=cw2, in1=ca[1][:, 0:511], op0=ALU.mult, op1=ALU.add),
                  [Rpb[pg], Rca[1], Rc], [Rca[2]])
            sc.op("dve", lambda h: h.scalar_tensor_tensor(out=ca[2][:, 511:512], in0=ph[:, 1:2], scalar=cw2, in1=ca[1][:, 511:512], op0=ALU.mult, op1=ALU.add),
                  [Rhalo[hi], Rca[1], Rc], [Rca[2]])
            sc.op("act", lambda h: h.activation(out=ca[3][:, :], in_=ca[2][:, :], func=AF.Silu), [Rca[2]], [Rca[3]])
            sc.op("dve", lambda h: h.tensor_tensor(out=aT[:, fc, :], in0=ca[3][:, :], in1=pb[pu][:, :], op=ALU.mult), [Rca[3], Rpb[pu]], [RaT[fc]])
        for db8 in range(8):
            di = c4["d"] % 2
            c4["d"] += 1
            sc.dma("sp", wds[di][:, :, :].rearrange("p f c -> p (f c)"), wdn_b[db8], [Rwdn[db8]], [Rwds[di]], ds_ld)
            for tt in range(4):
                pi = 4 + (tt + db8) % 2
                for fc in range(NFC):
                    sc.op("pe", lambda h, fc=fc: h.matmul(pb[pi][:, 0:256], lhsT=aT[:, fc, tt * 128:(tt + 1) * 128], rhs=wds[di][:, fc, :], start=(fc == 0), stop=(fc == NFC - 1)),
                          [RaT[fc], Rwds[di]], [Rpb[pi]], defer=(fc < NFC - 1))
                xs_ = x1g[gi2][:, tt, db8 * 256:(db8 + 1) * 256]
                sc.op("dve", lambda h: h.tensor_tensor(out=xs_, in0=pb[pi][:, 0:256], in1=xs_, op=ALU.add), [Rpb[pi], Rx1g[gi2][tt]], [Rx1g[gi2][tt]])
        for tt in range(4):
            et = 1 + g * 4 + tt
            yi = c4["y"] % 2
            c4["y"] += 1
            xs_ = x1g[gi2][:, tt, :]
            sc.op("act", lambda h: h.activation(out=junk5[:, :], in_=xs_, func=AF.Square, accum_out=sm5[:, 0:1]), [Rx1g[gi2][tt]], [Rj5, Rs5[0]])
            sc.op("dve", lambda h: h.tensor_scalar(out=sm5[:, 1:2], in0=sm5[:, 0:1], scalar1=1.0 / DM, scalar2=1e-6, op0=ALU.mult, op1=ALU.add), [Rs5[0]], [Rs5[1]])
            sc.op("act", lambda h: h.activation(out=sm5[:, 2:3], in_=sm5[:, 1:2], func=AF.Sqrt), [Rs5[1]], [Rs5[2]])
            sc.op("dve", lambda h: h.reciprocal(out=sm5[:, 3:4], in_=sm5[:, 2:3]), [Rs5[2]], [Rs5[3]])
            sc.op("dve", lambda h: h.scalar_tensor_tensor(out=yo[yi][:, :], in0=xs_, scalar=sm5[:, 3:4], in1=gf[:, :], op0=ALU.mult, op1=ALU.mult),
                  [Rx1g[gi2][tt], Rs5[3], Rgf], [Ryo[yi]])
            sc.dma("pool", y[(et - 1) * 128:et * 128, :], yo[yi][:, :], [Ryo[yi]], [Ry], ds_st)
    sc.barrier()
    es.close()
    return nc


def kernel(x, norm1_gain, w_in, rel_bias_table, lambda_q1, lambda_k1, lambda_q2, lambda_k2,
           diff_subln_gain, dil_out_gain, w_out, norm2_gain, w_gate_up, conv_w, conv_b,
           w_down, final_gain):
    f = np.float32
    x2 = np.asarray(x, f)[0]
    table = np.asarray(rel_bias_table, f)
    w_in0 = np.asarray(w_in, f)[0]
    blocks = np.ascontiguousarray(w_in0.reshape(16, 128, 48, 128).transpose(2, 1, 0, 3)).reshape(48, 128, 2048)
    wink = np.ascontiguousarray(np.concatenate([blocks[0:8], blocks[8:16], blocks[24:32], blocks[32:40]], 0))

    def vblk(cols):
        return np.ascontiguousarray(cols.reshape(16, 128, 2, 512).transpose(2, 1, 0, 3)).reshape(2, 128, 16 * 512)

    winv = vblk(w_in0[:, 2048:3072])
    winlv = vblk(w_in0[:, 5120:6144])
    wout_h = np.ascontiguousarray(np.asarray(w_out, f)[0].reshape(16, 128, DM).transpose(1, 0, 2)).reshape(128, 16 * DM)
    wgu_h = np.ascontiguousarray(np.asarray(w_gate_up, f)[0].reshape(16, 128, 88, 128).transpose(2, 1, 0, 3)).reshape(88, 128, 2048)
    wdn_h = np.ascontiguousarray(np.asarray(w_down, f)[0].reshape(NFC, 128, 8, 256).transpose(2, 1, 0, 3)).reshape(8, 128, NFC * 256)
    gains = np.ascontiguousarray(np.stack([
        np.broadcast_to(np.asarray(norm1_gain, f)[0], (128, DM)),
        np.broadcast_to(np.asarray(norm2_gain, f)[0], (128, DM)),
        np.broadcast_to(np.asarray(final_gain, f), (128, DM))], 0))
    MISC = 512 + 256 + 8 + NFC * 3 + NFC + NET
    ident = np.eye(128, dtype=f)
    in_maps = []
    for c in range(NCORE):
        tz, cbv, bm, kv = _host_tables(c, table)
        misc = np.zeros((128, MISC), f)
        misc[:, 0:128] = np.asarray(lambda_q1, f)[0][None]
        misc[:, 128:256] = np.asarray(lambda_k1, f)[0][None]
        misc[:, 256:384] = np.asarray(lambda_q2, f)[0][None]
        misc[:, 384:512] = np.asarray(lambda_k2, f)[0][None]
        misc[:, 512:768] = np.asarray(diff_subln_gain, f)[0][None]
        misc[:, 768:776] = np.asarray(dil_out_gain, f)[0].reshape(8, 128).T
        misc[:, 776:776 + NFC * 3] = np.asarray(conv_w, f)[0].T.reshape(NFC, 128, 3).transpose(1, 0, 2).reshape(128, NFC * 3)
        misc[:, 776 + NFC * 3:776 + NFC * 4] = np.asarray(conv_b, f)[0].reshape(NFC, 128).T
        tm = np.ones(NET, f)
        if c == 0:
            tm[0] = 0.0
        if c == NCORE - 1:
            tm[NET - 1] = 0.0
        misc[:, 776 + NFC * 4:] = tm[None]
        in_maps.append({
            "xr": np.ascontiguousarray(np.roll(x2, -OWN * c, axis=0)),
            "gains": gains, "wink": wink, "winv": winv, "winlv": winlv, "wout": wout_h,
            "wgu": wgu_h, "wdn": wdn_h,
            "tz": np.ascontiguousarray(tz.reshape(4, 128, 3 * TZW)), "cbv": cbv,
            "bm": np.ascontiguousarray(bm.reshape(8, 128, 768)), "kv": kv,
            "misc": misc, "ident": ident,
        })
    nc = build_nc()
    res = run_bass_kernel_spmd(nc, in_maps, core_ids=list(range(NCORE)))
    out = np.concatenate([np.asarray(res.results[c]["y"], f) for c in range(NCORE)], 0)
    return out[None].astype(np.float32)
```
